# Optimizing a Trainium2 kernel written in Bass

```python
import math
import jax, jax.numpy as jnp
from jax import lax
import numpy as np

D_MODEL = 1024
BATCH = 8
SEQ = 8192
DEPTH = 2

CHUNK = 64
RMS_EPS = 1e-6
N_BRANCHES = 3
GLA_HEADS = 4
GLA_DK = 128
GLA_DV = 128
GLA_GATE_RANK = 16
GLA_GATE_NORM = 16.0
GLA_KEY_W = GLA_HEADS * GLA_DK
GLA_VAL_W = GLA_HEADS * GLA_DV
ATT_HEADS = 8
ATT_HD = 64
ATT_W = ATT_HEADS * ATT_HD
ATT_LEFT_CHUNKS = 8
REL_MAX = 256
REL_SIZE = REL_MAX + CHUNK
GDN_HEADS = 4
GDN_DK = 128
GDN_DV = 128
GDN_KEY_W = GDN_HEADS * GDN_DK
GDN_VAL_W = GDN_HEADS * GDN_DV
GDN_CONV_CH = 2 * GDN_KEY_W + GDN_VAL_W
CONV_W = 4
FFN_HIDDEN = -(-(8 * D_MODEL) // (3 * 256)) * 256
IN_WIDTHS = (GLA_KEY_W, GLA_KEY_W, GLA_VAL_W, GLA_VAL_W, GLA_GATE_RANK,
             ATT_W, ATT_W, ATT_W,
             GDN_KEY_W, GDN_KEY_W, GDN_VAL_W, GDN_VAL_W, GDN_HEADS, GDN_HEADS,
             N_BRANCHES * D_MODEL)
IN_W = sum(IN_WIDTHS)

kernel_name = 'hybrid_gla_chunkattn_gdn_block'

F32 = jnp.float32


def rmsnorm(x, g):
    xf = x.astype(F32)
    y = xf * lax.rsqrt(jnp.mean(xf * xf, axis=-1, keepdims=True) + RMS_EPS)
    return (y * g.astype(F32)).astype(x.dtype)


def gated_head_rmsnorm(o, gate, g, n_heads):
    b, s, w = o.shape
    oh = o.reshape(b, s, n_heads, w // n_heads).astype(F32)
    oh = oh * lax.rsqrt(jnp.mean(oh * oh, axis=-1, keepdims=True) + RMS_EPS) * g.astype(F32)
    return oh.reshape(b, s, w) * jax.nn.silu(gate.astype(F32))


def l2norm(t):
    return t * lax.rsqrt(jnp.sum(t * t, axis=-1, keepdims=True) + RMS_EPS)


def to_chunks(t, n_heads):
    b, s, w = t.shape
    return t.reshape(b, s // CHUNK, CHUNK, n_heads, w // n_heads).transpose(0, 3, 1, 2, 4)


def from_chunks(t):
    b, h, n, c, d = t.shape
    return t.transpose(0, 2, 3, 1, 4).reshape(b, n * c, h * d)


def causal_depthwise_conv(x, w):
    return lax.conv_general_dilated(
        x, w[:, None, :].astype(x.dtype), window_strides=(1,), padding=[(CONV_W - 1, 0)],
        dimension_numbers=('NWC', 'WIO', 'NWC'), feature_group_count=x.shape[-1])


def gla_mixer(q, k, v, r, gate_lr, w_gate_up, b_gate, norm_g):
    b, s, _ = q.shape
    log_a = jax.nn.log_sigmoid((gate_lr @ w_gate_up + b_gate).astype(F32)) / GLA_GATE_NORM
    qc = to_chunks(q.astype(F32) * GLA_DK ** -0.5, GLA_HEADS)
    kc = to_chunks(k.astype(F32), GLA_HEADS)
    vc = to_chunks(v.astype(F32), GLA_HEADS)
    gc = jnp.cumsum(to_chunks(log_a, GLA_HEADS), axis=3)
    causal = jnp.tril(jnp.ones((CHUNK, CHUNK), dtype=bool))

    def step(state, inp):
        q_, k_, v_, g_ = inp
        diff = jnp.where(causal[:, :, None], g_[:, :, :, None, :] - g_[:, :, None, :, :], -jnp.inf)
        scores = jnp.einsum('bhik,bhijk->bhij', q_, jnp.exp(diff) * k_[:, :, None, :, :])
        g_last = g_[:, :, -1, :]
        o = (jnp.einsum('bhik,bhkv->bhiv', q_ * jnp.exp(g_), state)
             + jnp.einsum('bhij,bhjv->bhiv', scores, v_))
        state = (jnp.exp(g_last)[..., None] * state
                 + jnp.einsum('bhjk,bhjv->bhkv', k_ * jnp.exp(g_last[:, :, None, :] - g_), v_))
        return state, o

    init = jnp.zeros((b, GLA_HEADS, GLA_DK, GLA_DV), F32)
    xs = (jnp.moveaxis(qc, 2, 0), jnp.moveaxis(kc, 2, 0), jnp.moveaxis(vc, 2, 0), jnp.moveaxis(gc, 2, 0))
    _, o = lax.scan(step, init, xs)
    o = from_chunks(jnp.moveaxis(o, 0, 2))
    return gated_head_rmsnorm(o, r, norm_g, GLA_HEADS).astype(q.dtype)


def chunk_band_attention(q, k, v, rel_bias):
    b, s, _ = q.shape
    n = s // CHUNK
    left = ATT_LEFT_CHUNKS * CHUNK
    band = left + CHUNK

    def heads(t):
        return t.reshape(b, s, ATT_HEADS, ATT_HD).transpose(0, 2, 1, 3)

    qh = heads(q) * ATT_HD ** -0.5
    kh = jnp.pad(heads(k), ((0, 0), (0, 0), (left, 0), (0, 0)))
    vh = jnp.pad(heads(v), ((0, 0), (0, 0), (left, 0), (0, 0)))
    rel = jnp.arange(CHUNK)[:, None] + left - jnp.arange(band)[None, :]
    bias = rel_bias[:, jnp.clip(rel, -(CHUNK - 1), REL_MAX) + (CHUNK - 1)].astype(F32)

    def one_chunk(c):
        start = c * CHUNK
        q_ = lax.dynamic_slice_in_dim(qh, start, CHUNK, axis=2)
        k_ = lax.dynamic_slice_in_dim(kh, start, band, axis=2)
        v_ = lax.dynamic_slice_in_dim(vh, start, band, axis=2)
        scores = jnp.einsum('bhid,bhjd->bhij', q_, k_, preferred_element_type=F32) + bias
        valid = (start - left + jnp.arange(band)) >= 0
        p = jax.nn.softmax(jnp.where(valid, scores, -jnp.inf), axis=-1)
        return jnp.einsum('bhij,bhjd->bhid', p.astype(v_.dtype), v_)

    o = lax.map(one_chunk, jnp.arange(n))
    return o.transpose(1, 0, 3, 2, 4).reshape(b, s, ATT_W)


def gated_deltanet(q, k, v, z, a, b_logit, conv_w, a_log, dt_bias, norm_g):
    out_dtype = q.dtype
    qkv = causal_depthwise_conv(jnp.concatenate([q, k, v], axis=-1), conv_w)
    qkv = jax.nn.silu(qkv.astype(F32))
    q, k, v = jnp.split(qkv, [GDN_KEY_W, 2 * GDN_KEY_W], axis=-1)
    log_alpha = -jnp.exp(a_log.astype(F32)) * jax.nn.softplus(a.astype(F32) + dt_bias.astype(F32))
    beta = jax.nn.sigmoid(b_logit.astype(F32))
    qc = l2norm(to_chunks(q, GDN_HEADS)) * GDN_DK ** -0.5
    kc = l2norm(to_chunks(k, GDN_HEADS))
    vc = to_chunks(v, GDN_HEADS)
    gc = jnp.cumsum(to_chunks(log_alpha, GDN_HEADS)[..., 0], axis=-1)
    bc = to_chunks(beta, GDN_HEADS)
    incl = jnp.tril(jnp.ones((CHUNK, CHUNK), dtype=bool))
    strict = jnp.tril(jnp.ones((CHUNK, CHUNK), dtype=bool), k=-1)
    gamma = jnp.exp(jnp.where(incl, gc[..., :, None] - gc[..., None, :], -jnp.inf))
    k_beta = kc * bc
    a_mat = jnp.where(strict, jnp.einsum('bhnid,bhnjd->bhnij', k_beta, kc) * gamma, 0.0)
    rhs = jnp.concatenate([vc * bc, k_beta * jnp.exp(gc)[..., None]], axis=-1)
    sol = lax.linalg.triangular_solve(a_mat + jnp.eye(CHUNK, dtype=F32), rhs,
                                      left_side=True, lower=True, unit_diagonal=True)
    u = sol[..., :GDN_DV]
    w = sol[..., GDN_DV:]
    attn_qk = jnp.where(incl, jnp.einsum('bhnid,bhnjd->bhnij', qc, kc) * gamma, 0.0)

    def step(state, inp):
        q_, k_, u_, w_, g_, a_ = inp
        v_new = u_ - jnp.einsum('bhik,bhkv->bhiv', w_, state)
        o = (jnp.einsum('bhik,bhkv->bhiv', q_ * jnp.exp(g_)[..., None], state)
             + jnp.einsum('bhij,bhjv->bhiv', a_, v_new))
        g_last = g_[..., -1]
        state = (state * jnp.exp(g_last)[..., None, None]
                 + jnp.einsum('bhjk,bhjv->bhkv', k_ * jnp.exp(g_last[..., None] - g_)[..., None], v_new))
        return state, o

    bsz = qc.shape[0]
    init = jnp.zeros((bsz, GDN_HEADS, GDN_DK, GDN_DV), F32)
    xs = tuple(jnp.moveaxis(t, 2, 0) for t in (qc, kc, u, w, gc, attn_qk))
    _, o = lax.scan(step, init, xs)
    o = from_chunks(jnp.moveaxis(o, 0, 2))
    return gated_head_rmsnorm(o, z, norm_g, GDN_HEADS).astype(out_dtype)


def split_columns(t, widths):
    bounds, acc = [], 0
    for wdt in widths[:-1]:
        acc += wdt
        bounds.append(acc)
    return jnp.split(t, bounds, axis=-1)


def hybrid_layer(x, mix_norm_g, w_in, gla_w_gate_up, gla_b_gate, gla_norm_g, att_rel_bias,
                 gdn_conv_w, gdn_a_log, gdn_dt_bias, gdn_norm_g, w_branch_gla, w_branch_att,
                 w_branch_gdn, w_out, ffn_norm_g, w_ffn_in, w_ffn_out):
    b, s, _ = x.shape
    h = rmsnorm(x, mix_norm_g)
    proj = h @ w_in
    (gla_q, gla_k, gla_v, gla_r, gla_lr, att_q, att_k, att_v,
     gdn_q, gdn_k, gdn_v, gdn_z, gdn_a, gdn_b, merge) = split_columns(proj, IN_WIDTHS)
    o_gla = gla_mixer(gla_q, gla_k, gla_v, gla_r, gla_lr, gla_w_gate_up, gla_b_gate, gla_norm_g)
    o_att = chunk_band_attention(att_q, att_k, att_v, att_rel_bias)
    o_gdn = gated_deltanet(gdn_q, gdn_k, gdn_v, gdn_z, gdn_a, gdn_b, gdn_conv_w, gdn_a_log,
                           gdn_dt_bias, gdn_norm_g)
    gates = jax.nn.sigmoid(merge.astype(F32)).reshape(b, s, N_BRANCHES, D_MODEL)
    y = (gates[:, :, 0] * (o_gla @ w_branch_gla).astype(F32)
         + gates[:, :, 1] * (o_att @ w_branch_att).astype(F32)
         + gates[:, :, 2] * (o_gdn @ w_branch_gdn).astype(F32))
    x = x + y.astype(x.dtype) @ w_out
    h2 = rmsnorm(x, ffn_norm_g)
    gate, up = jnp.split(h2 @ w_ffn_in, 2, axis=-1)
    return x + (jax.nn.silu(gate) * up) @ w_ffn_out


def setup_inputs(seed: int = 0) -> dict:
    key = jax.random.key(seed)
    ks = jax.random.split(key, 20)

    def nrm(k, shape, fan_in):
        return jax.random.normal(k, shape, F32) * fan_in ** -0.5

    def gain(k, shape):
        return 1.0 + 0.01 * jax.random.normal(k, shape, F32)

    dt = jnp.exp(jax.random.uniform(ks[9], (DEPTH, GDN_HEADS), F32)
                 * (math.log(0.1) - math.log(0.001)) + math.log(0.001))
    return {
        'x': jax.random.normal(ks[0], (BATCH, SEQ, D_MODEL), F32),
        'mix_norm_g': gain(ks[1], (DEPTH, D_MODEL)),
        'w_in': nrm(ks[2], (DEPTH, D_MODEL, IN_W), D_MODEL),
        'gla_w_gate_up': nrm(ks[3], (DEPTH, GLA_GATE_RANK, GLA_KEY_W), GLA_GATE_RANK),
        'gla_b_gate': 0.1 * jax.random.normal(ks[4], (DEPTH, GLA_KEY_W), F32),
        'gla_norm_g': gain(ks[5], (DEPTH, GLA_DV)),
        'att_rel_bias': 0.1 * jax.random.normal(ks[6], (DEPTH, ATT_HEADS, REL_SIZE), F32),
        'gdn_conv_w': nrm(ks[7], (DEPTH, CONV_W, GDN_CONV_CH), CONV_W),
        'gdn_a_log': jnp.log(jax.random.uniform(ks[8], (DEPTH, GDN_HEADS), F32, 1.0, 16.0)),
        'gdn_dt_bias': dt + jnp.log(-jnp.expm1(-dt)),
        'gdn_norm_g': gain(ks[10], (DEPTH, GDN_DV)),
        'w_branch_gla': nrm(ks[11], (DEPTH, GLA_VAL_W, D_MODEL), GLA_VAL_W),
        'w_branch_att': nrm(ks[12], (DEPTH, ATT_W, D_MODEL), ATT_W),
        'w_branch_gdn': nrm(ks[13], (DEPTH, GDN_VAL_W, D_MODEL), GDN_VAL_W),
        'w_out': nrm(ks[14], (DEPTH, D_MODEL, D_MODEL), D_MODEL),
        'ffn_norm_g': gain(ks[15], (DEPTH, D_MODEL)),
        'w_ffn_in': nrm(ks[16], (DEPTH, D_MODEL, 2 * FFN_HIDDEN), D_MODEL),
        'w_ffn_out': nrm(ks[17], (DEPTH, FFN_HIDDEN, D_MODEL), FFN_HIDDEN),
        'final_norm_g': gain(ks[18], (D_MODEL,)),
    }


def reference(x, mix_norm_g, w_in, gla_w_gate_up, gla_b_gate, gla_norm_g, att_rel_bias,
              gdn_conv_w, gdn_a_log, gdn_dt_bias, gdn_norm_g, w_branch_gla, w_branch_att,
              w_branch_gdn, w_out, ffn_norm_g, w_ffn_in, w_ffn_out, final_norm_g):
    for l in range(DEPTH):
        x = hybrid_layer(x, mix_norm_g[l], w_in[l], gla_w_gate_up[l], gla_b_gate[l], gla_norm_g[l],
                         att_rel_bias[l], gdn_conv_w[l], gdn_a_log[l], gdn_dt_bias[l], gdn_norm_g[l],
                         w_branch_gla[l], w_branch_att[l], w_branch_gdn[l], w_out[l], ffn_norm_g[l],
                         w_ffn_in[l], w_ffn_out[l])
    return rmsnorm(x, final_norm_g)
```

```python
import numpy as np
from contextlib import ExitStack
import concourse.bass as bass
import concourse.mybir as mybir
from concourse.bass_utils import run_bass_kernel_spmd

F32 = mybir.dt.float32
BF16 = mybir.dt.bfloat16
AF = mybir.ActivationFunctionType
ALU = mybir.AluOpType

D = 1024
T = 512
KC = 8
INW = 8728
FH = 2816
DEPTH = 2
C_GQ, C_GK, C_GV, C_GR, C_LR = 0, 512, 1024, 1536, 2048
C_AQ, C_AK, C_AV = 2064, 2576, 3088
C_DQ, C_DK, C_DV, C_DZ, C_AB, C_MG = 3600, 4112, 4624, 5136, 5648, 5656
NEG = -30000.0
EPS = 1e-6
WST = 4096

K_ID, K_UT, K_NI4, K_NS4, K_RM, K_ONE = 0, 128, 256, 768, 1280, 1792
CW = 1920
P_MIXG, P_FFNG, P_FING, P_GLAB, P_GLANG, P_GDNNG, P_CONV, P_ALOG, P_DTB = 0, 16, 32, 40, 48, 50, 52, 148, 156
PW = 164


class Reg:
    __slots__ = ("ap", "w", "r", "excl")

    def __init__(self, ap, excl=False):
        self.ap = ap
        self.w = None
        self.r = {}
        self.excl = excl

    def __getitem__(self, idx):
        return V(self, self.ap[idx])

    @property
    def v(self):
        return V(self, self.ap)


class V:
    __slots__ = ("reg", "ap")

    def __init__(self, reg, ap):
        self.reg = reg
        self.ap = ap

    def __getitem__(self, idx):
        return V(self.reg, self.ap[idx])

    @property
    def v(self):
        return self

    def re(self, s, **kw):
        return V(self.reg, self.ap.rearrange(s, **kw))

    def bc(self, shape):
        return V(self.reg, self.ap.to_broadcast(shape))

    def un(self, axis):
        return V(self.reg, self.ap.unsqueeze(axis))


WKEYS = ("out", "accum_out", "ap")
import os as _os
EAGER_INC = _os.environ.get("EAGER_INC", "1") == "1"
GDN_LEAD = int(_os.environ.get("GDN_LEAD", "0"))
NEU_DT = mybir.dt.float32r if _os.environ.get("NEU_F32R", "0") == "1" else F32


class Sched:
    def __init__(self, nc, stack, ndsem=40):
        self.nc = nc
        self.stack = stack
        self.once = []
        self.engs = {"pe": nc.tensor, "act": nc.scalar, "dve": nc.vector, "pool": nc.gpsimd, "sp": nc.sync}
        self.sem = {k: stack.enter_context(nc.semaphore("s_" + k)) for k in self.engs}
        self.seq = {k: 0 for k in self.engs}
        self.last = {k: None for k in self.engs}
        self.incs = {k: [] for k in self.engs}
        self.waited = {k: {} for k in self.engs}
        self.dsems = [stack.enter_context(nc.semaphore("d%d" % i)) for i in range(ndsem)]
        self.dcnt = [0] * ndsem
        self.dnext = 0
        self.ninstr = 0
        self.nwait = 0
        self.ninc = 0

    def _resolve(self, tok):
        if tok[0] == "d":
            return tok[1], tok[2]
        e, q = tok[1], tok[2]
        lst = self.incs[e]
        import bisect
        i = bisect.bisect_left(lst, q)
        if i == len(lst):
            self.last[e].then_inc(self.sem[e], 1)
            lst.append(self.seq[e])
            self.ninc += 1
        return self.sem[e], i + 1

    def _wait(self, e, raw, other):
        best = {}
        for lst, is_raw in ((raw, True), (other, False)):
            for d in lst:
                if d is None:
                    continue
                if d[0] == "e" and d[1] == e and (e == "pe" or not is_raw):
                    continue
                k = ("d", id(d[1])) if d[0] == "d" else ("e", d[1])
                if k not in best or best[k][2] < d[2]:
                    best[k] = d
        eng = self.engs[e]
        for k, d in best.items():
            if d[0] == "e":
                if self.waited[e].get(k, 0) >= d[2]:
                    continue
                s, v = self._resolve(d)
                self.waited[e][k] = self.incs[d[1]][v - 1]
            else:
                if self.waited[e].get(k, 0) >= d[2]:
                    continue
                s, v = d[1], d[2]
                self.waited[e][k] = v
            eng.wait_ge(s, v)
            self.nwait += 1

    def _collect(self, reads, writes):
        raw = [b.w for b in reads]
        other = []
        for b in reads:
            if b.excl:
                other.extend(b.r.values())
        for b in writes:
            other.append(b.w)
            other.extend(b.r.values())
        return raw, other

    def _commit(self, key, tok, reads, writes):
        for b in reads:
            b.r[key] = tok
        for b in writes:
            b.w = tok
            b.r = {}

    def I(self, e, meth, xr=(), xw=(), **kw):
        reads, writes, args = list(xr), list(xw), {}
        for k, v in kw.items():
            if isinstance(v, V):
                (writes if k in WKEYS else reads).append(v.reg)
                args[k] = v.ap
            else:
                args[k] = v
        raw, other = self._collect(reads, writes)
        self._wait(e, raw, other)
        ins = getattr(self.engs[e], meth)(**args)
        self.seq[e] += 1
        self.last[e] = ins
        self.ninstr += 1
        if EAGER_INC and (e != "pe" or kw.get("stop", True)):
            ins.then_inc(self.sem[e], 1)
            self.incs[e].append(self.seq[e])
            self.ninc += 1
        self._commit(e, ("e", e, self.seq[e]), reads, writes)

    def dma(self, e, out, in_, xr=(), xw=()):
        reads, writes = list(xr), list(xw)
        if isinstance(out, V):
            if out.reg is not None:
                writes.append(out.reg)
            out = out.ap
        if isinstance(in_, V):
            if in_.reg is not None:
                reads.append(in_.reg)
            in_ = in_.ap
        i = self.dnext
        self.dnext = (self.dnext + 1) % len(self.dsems)
        s = self.dsems[i]
        raw, other = self._collect(reads, writes)
        other.append(("d", s, 16 * self.dcnt[i]))
        self._wait(e, raw, other)
        self.dcnt[i] += 1
        self.engs[e].dma_start(out=out, in_=in_).then_inc(s, 16)
        self.ninstr += 1
        self._commit("dma%d" % i, ("d", s, 16 * self.dcnt[i]), reads, writes)

    def dma_once(self, e, out, in_, xr=(), xw=()):
        reads, writes = list(xr), list(xw)
        if isinstance(out, V):
            if out.reg is not None:
                writes.append(out.reg)
            out = out.ap
        if isinstance(in_, V):
            if in_.reg is not None:
                reads.append(in_.reg)
            in_ = in_.ap
        s = self.stack.enter_context(self.nc.semaphore("o%d" % len(self.once)))
        self.once.append(s)
        raw, other = self._collect(reads, writes)
        self._wait(e, raw, other)
        self.engs[e].dma_start(out=out, in_=in_).then_inc(s, 16)
        self.ninstr += 1
        self._commit("once%d" % len(self.once), ("d", s, 16), reads, writes)

    def fresh(self, regs):
        snap = {e: ("e", e, self.seq[e]) for e in self.engs if self.seq[e] > 0}
        for i, sm in enumerate(self.dsems):
            if self.dcnt[i]:
                snap["dma%d" % i] = ("d", sm, 16 * self.dcnt[i])
        for i, sm in enumerate(self.once):
            snap["once%d" % (i + 1)] = ("d", sm, 16)
        for r in regs:
            r.w = None
            r.r = dict(snap)

    def finish(self):
        for i, s in enumerate(self.dsems):
            if self.dcnt[i]:
                self.nc.sync.wait_ge(s, 16 * self.dcnt[i])


def build(NST, depth=DEPTH, dbg=(), upto=99):
    nc = bass.Bass("TRN2", target_bir_lowering=False)
    NTOK = NST * T

    def din(name, shape, dt=F32):
        return nc.dram_tensor(name, list(shape), dt, kind="ExternalInput").ap()

    x_d = din("x", [NTOK, D])
    w_in_d = din("w_in", [depth, D, INW])
    w_bg_d = din("w_branch_gla", [depth, 512, D])
    w_ba_d = din("w_branch_att", [depth, 512, D])
    w_bd_d = din("w_branch_gdn", [depth, 512, D])
    w_out_d = din("w_out", [depth, D, D])
    w_fi_d = din("w_ffn_in", [depth, D, 2 * FH])
    w_fo_d = din("w_ffn_out", [depth, FH, D])
    wgu_d = din("wgu", [16, depth * 512])
    cst_d = din("cst", [128, CW])
    par_d = din("par", [128, PW])
    bm_d = din("bm", [depth, 128, 8 * 640])
    out_d = nc.dram_tensor("out", [NTOK, D], F32, kind="ExternalOutput").ap()

    def dscr(name, shape):
        return nc.dram_tensor(name, list(shape), BF16, kind="Internal").ap()

    w_in_b = dscr("w_in_b", [depth, D, INW])
    w_bg_b = dscr("w_bg_b", [depth, 512, D])
    w_ba_b = dscr("w_ba_b", [depth, 512, D])
    w_bd_b = dscr("w_bd_b", [depth, 512, D])
    w_out_b = dscr("w_out_b", [depth, D, D])
    w_fi_b = dscr("w_fi_b", [depth, D, 2 * FH])
    w_fo_b = dscr("w_fo_b", [depth, FH, D])

    dbg_out = {}
    marks = []
    build.marks = marks

    with ExitStack() as st0:
        S = Sched(nc, st0)
        ncount = [0]

        freed = []

        def _merge(dst, tok):
            if tok is None:
                return
            k = ("d", id(tok[1])) if tok[0] == "d" else ("e", tok[1])
            if k not in dst or dst[k][2] < tok[2]:
                dst[k] = tok

        def sb(stack, shape, dt, fresh=True):
            ncount[0] += 1
            t = stack.enter_context(nc.sbuf_tensor("t%d" % ncount[0], list(shape), dt))
            r = Reg(t[tuple(slice(None) for _ in shape)])
            if fresh:
                ml = nc.lookup_mloc(t)
                a0, a1 = ml.addr, ml.addr + ml.dims[1]
                deps = {}
                keep = []
                for f in freed:
                    if f[0] < a1 and a0 < f[1]:
                        for tok in f[2].values():
                            _merge(deps, tok)
                        if a0 <= f[0] and f[1] <= a1:
                            continue
                    keep.append(f)
                freed[:] = keep
                r.r = deps

                def retire(r=r, a0=a0, a1=a1):
                    d = {}
                    _merge(d, r.w)
                    for tok in r.r.values():
                        _merge(d, tok)
                    freed.append([a0, a1, d])

                stack.callback(retire)
            return r

        def gsb(shape, dt):
            return sb(st0, shape, dt, fresh=False)

        def dump(name, v, shape):
            if name not in dbg:
                return
            key = "dbg_%s_%d" % (name, len([k for k in dbg_out if k.startswith("dbg_" + name + "_")]))
            dt = v.ap.dtype
            d = nc.dram_tensor(key, list(shape), dt, kind="ExternalOutput").ap()
            dbg_out[key] = d
            S.dma("sp", out=d, in_=v)

        psA = st0.enter_context(nc.psum_tensor("psA", [128, 1024], F32))
        psB = st0.enter_context(nc.psum_tensor("psB", [128, 1024], F32))
        ps4 = st0.enter_context(nc.psum_tensor("ps4", [128, 512], F32))
        ps5 = st0.enter_context(nc.psum_tensor("ps5", [128, 512], F32))
        ps6 = st0.enter_context(nc.psum_tensor("ps6", [128, 512], F32))
        ps7 = st0.enter_context(nc.psum_tensor("ps7", [128, 1024], BF16))
        ACC = [Reg(psA[:, 0:512], True), Reg(psA[:, 512:1024], True), Reg(psB[:, 0:512], True), Reg(psB[:, 512:1024], True),
               Reg(ps4[:, :], True), Reg(ps5[:, :], True)]
        import os
        if os.environ.get("ACC45") == "1":
            ACC = [ACC[4], ACC[5]]
            acc_state_list = [0, 1]
        SC = [Reg(psA[:, 0:640], True), Reg(psB[:, 0:640], True)]
        MSr = Reg(ps6[:, :], True)
        MS = [MSr[:, i * 128:(i + 1) * 128] for i in range(4)]
        TBr = Reg(ps7[:, :], True)
        TB = [TBr[:, 0:512], TBr[:, 512:1024]]
        acc_state = {"list": list(range(len(ACC))), "i": 0, "m": 0, "t": 0}

        def acc():
            lst = acc_state["list"]
            r = ACC[lst[acc_state["i"] % len(lst)]]
            acc_state["i"] += 1
            return r

        def msm():
            r = MS[acc_state["m"] % 4]
            acc_state["m"] += 1
            return r

        def tbb():
            r = TB[acc_state["t"] % 2]
            acc_state["t"] += 1
            return r

        cst = gsb([128, CW], F32)
        par = gsb([128, PW], F32)
        ident32 = cst[:, K_ID:K_ID + 128]
        ut32 = cst[:, K_UT:K_UT + 128]
        negi4 = cst[:, K_NI4:K_NI4 + 512]
        negs4 = cst[:, K_NS4:K_NS4 + 512]
        rmask = cst[:, K_RM:K_RM + 512]
        ones32 = cst[:, K_ONE:K_ONE + 128]
        cbf = gsb([128, 256], BF16)
        ident_bf = cbf[:, 0:128]
        ones_bf = cbf[:, 128:256]
        der = gsb([128, 16], F32)
        wgu = gsb([16, depth * 512], BF16)
        xTt = st0.enter_context(nc.sbuf_tensor("xT", [128, KC, T], F32))
        xT = [Reg(xTt[:, k, :]) for k in range(KC)]
        hTt = st0.enter_context(nc.sbuf_tensor("hT", [128, KC, T], BF16))
        hT = [Reg(hTt[:, k, :]) for k in range(KC)]
        NWST = 5
        wst = [gsb([128, WST], BF16) for _ in range(NWST)]
        wst_i = [0]
        sqb = [gsb([128, T], BF16) for _ in range(2)]
        rs = gsb([128, T], F32)
        bmh = [gsb([128, 640], F32) for _ in range(2)]
        ogla = gsb([128, 4, T], BF16)
        oatt = gsb([128, 4, T], BF16)
        ogdn = gsb([128, 4, T], BF16)
        kTh = [gsb([128, 4, 2 * T], BF16) for _ in range(depth)]
        vh = [gsb([128, 8, 8 * 65], BF16) for _ in range(depth)]
        Sgla = [gsb([128, 512], F32) for _ in range(depth)]
        Sgla_b = [gsb([128, 512], BF16) for _ in range(depth)]
        Sgdn = [gsb([128, 512], F32) for _ in range(depth)]
        Sgdn_b = [gsb([128, 512], BF16) for _ in range(depth)]
        chist = [gsb([128, 12, 4], BF16) for _ in range(depth)]

        S.dma("sp", out=cst.v, in_=cst_d)
        S.dma("sp", out=par.v, in_=par_d)
        S.dma_once("pool", out=wgu.v, in_=wgu_d)
        S.I("dve", "tensor_copy", out=cbf[:, 0:128], in_=ident32)
        S.I("dve", "tensor_copy", out=cbf[:, 128:256], in_=ones32)
        S.I("dve", "tensor_scalar", out=der[:, 0:8], in0=par[:, P_GLAB:P_GLAB + 8], scalar1=-1.0, scalar2=None, op0=ALU.mult)
        S.I("act", "activation", out=der[:, 8:16], in_=par[:, P_ALOG:P_ALOG + 8], func=AF.Exp)
        S.I("dve", "tensor_scalar", out=der[:, 8:16], in0=der[:, 8:16], scalar1=-1.0, scalar2=None, op0=ALU.mult)
        for l in range(depth):
            S.I("pool", "memset", ap=vh[l].v, constant=1.0)
            S.I("pool", "memset", ap=Sgla[l].v, constant=0.0)
            S.I("pool", "memset", ap=Sgla_b[l].v, constant=0.0)
            S.I("pool", "memset", ap=Sgdn[l].v, constant=0.0)
            S.I("pool", "memset", ap=Sgdn_b[l].v, constant=0.0)
            S.I("pool", "memset", ap=chist[l].v, constant=0.0)
            S.I("pool", "memset", ap=kTh[l].v, constant=0.0)

        wb = {}

        def cast(name, src, dst, rows):
            for l in range(depth):
                b = Reg(None)
                S.dma_once("pool", out=dst[l], in_=src[l], xw=[b])
                wb[(name, l)] = [b]

        import os
        NOCAST = os.environ.get("NOCAST") == "1"
        if NOCAST:
            def cast(name, src, dst, rows):
                for l in range(depth):
                    wb[(name, l)] = []
        cast("in", w_in_d, w_in_b, D)
        cast("bg", w_bg_d, w_bg_b, 512)
        cast("ba", w_ba_d, w_ba_b, 512)
        cast("bd", w_bd_d, w_bd_b, 512)
        cast("out", w_out_d, w_out_b, D)
        cast("fi", w_fi_d, w_fi_b, D)
        cast("fo", w_fo_d, w_fo_b, FH)

        def load(name, l, view, a, n):
            r = wst[wst_i[0] % NWST]
            wst_i[0] += 1
            dst = r[:, 0:a * n].re("p (a n) -> p a n", a=a)
            S.dma("sp", out=dst, in_=view, xr=wb[(name, l)])
            return dst

        def load_in(l, c0, n):
            return load("in", l, w_in_b[l].rearrange("(k p) c -> p k c", p=128)[:, :, c0:c0 + n], KC, n)

        def proj_fm(a, wv, m=128):
            for kc in range(KC):
                S.I("pe", "matmul", out=a[0:m, :], lhsT=wv[:, kc, :], rhs=hT[kc].v, start=(kc == 0), stop=(kc == KC - 1))

        def proj_tm(a, tt, wv, n):
            for kc in range(KC):
                S.I("pe", "matmul", out=a[:, 0:n], lhsT=hT[kc][:, tt * 128:(tt + 1) * 128], rhs=wv[:, kc, :],
                    start=(kc == 0), stop=(kc == KC - 1))

        def rmsnorm(gcol0):
            a = acc()
            for kc in range(KC):
                q = sqb[kc % 2]
                S.I("act", "activation", out=q.v, in_=xT[kc].v, func=AF.Square)
                S.I("pe", "matmul", out=a.v, lhsT=ones_bf, rhs=q.v, start=(kc == 0), stop=(kc == KC - 1))
            S.I("act", "activation", out=rs.v, in_=a.v, func=AF.Sqrt, scale=1.0 / D, bias=EPS)
            S.I("dve", "reciprocal", out=rs.v, in_=rs.v)

        def headnorm(raw, gate_v, gcol, out_v, accfn=None):
            q = sqb[0]
            a = (accfn or acc)()
            S.I("act", "activation", out=q.v, in_=raw.v, func=AF.Square)
            S.I("pe", "matmul", out=a.v, lhsT=ones_bf, rhs=q.v, start=True, stop=True)
            S.I("act", "activation", out=rs.v, in_=a.v, func=AF.Sqrt, scale=1.0 / 128, bias=EPS)
            S.I("dve", "reciprocal", out=rs.v, in_=rs.v)
            S.I("dve", "scalar_tensor_tensor", out=rs.v, in0=raw.v, scalar=gcol, in1=rs.v, op0=ALU.mult, op1=ALU.mult)
            S.I("pool", "tensor_tensor", out=out_v, in0=rs.v.re("p (h t) -> p h t", h=4), in1=gate_v, op=ALU.mult)

        def gla_phase(l, st):
            with ExitStack() as ph:
                lrT = sb(ph, [16, T], BF16)
                gb = [[sb(ph, [128, T], F32) for _ in range(5)] for _ in range(2)]
                qg = sb(ph, [128, 4, T], BF16)
                kg = sb(ph, [128, 4, T], BF16)
                vtok = [sb(ph, [128, 512], BF16) for _ in range(4)]
                rsl = sb(ph, [128, 4, T], BF16)
                egl = sb(ph, [128, 16], F32)
                sTm = [sb(ph, [128, 4, 128], BF16) for _ in range(2)]
                kkT = [sb(ph, [128, 4, 128], BF16) for _ in range(2)]
                kkt = [sb(ph, [128, 4, 128], BF16) for _ in range(2)]

                wl = load_in(l, C_LR, 16)
                a = acc()
                proj_fm(a, wl, m=16)
                S.I("act", "activation", out=lrT.v, in_=a[0:16, :], func=AF.Copy)
                wq = load_in(l, C_GQ, 512)
                wk = load_in(l, C_GK, 512)
                wv = load_in(l, C_GV, 512)
                wr = load_in(l, C_GR, 512)
                gpool = [0, 1]

                def gate_gen(h):
                    while not gpool:
                        yield
                    sl = gpool.pop(0)
                    t1, t2, Gs, eg, eng = (gb[sl][k] for k in range(5))
                    a = acc()
                    S.I("pe", "matmul", out=a.v, lhsT=wgu[0:16, l * 512 + h * 128:l * 512 + (h + 1) * 128], rhs=lrT.v,
                        start=True, stop=True)
                    S.I("act", "activation", out=t1.v, in_=a.v, func=AF.Exp, scale=-1.0, bias=der[:, l * 4 + h:l * 4 + h + 1])
                    yield
                    S.I("act", "activation", out=t2.v, in_=t1.v, func=AF.Ln, bias=1.0)
                    yield
                    S.I("dve", "tensor_tensor_scan", out=Gs.v, data0=rmask, data1=t2.v, initial=0.0, op0=ALU.mult, op1=ALU.add)
                    yield
                    S.I("act", "activation", out=eg.v, in_=Gs.v, func=AF.Exp, scale=-1.0 / 16)
                    S.I("act", "activation", out=eng.v, in_=Gs.v, func=AF.Exp, scale=1.0 / 16)
                    S.I("act", "activation", out=egl[:, h * 4:(h + 1) * 4], in_=Gs.v.re("p (t c) -> p t c", c=128)[:, :, 127],
                        func=AF.Exp, scale=-1.0 / 16)
                    yield
                    a = acc()
                    proj_fm(a, wq[:, :, h * 128:(h + 1) * 128])
                    S.I("dve", "scalar_tensor_tensor", out=qg[:, h, :], in0=a.v, scalar=128 ** -0.5, in1=eg.v, op0=ALU.mult, op1=ALU.mult)
                    yield
                    a = acc()
                    proj_fm(a, wk[:, :, h * 128:(h + 1) * 128])
                    S.I("dve", "tensor_tensor", out=kg[:, h, :], in0=a.v, in1=eng.v, op=ALU.mult)
                    gpool.append(sl)

                def v_gen(tt):
                    a = acc()
                    proj_tm(a, tt, wv, 512)
                    S.I("act", "activation", out=vtok[tt].v, in_=a.v, func=AF.Copy)
                    yield

                def r_gen(h):
                    a = acc()
                    proj_fm(a, wr[:, :, h * 128:(h + 1) * 128])
                    S.I("act", "activation", out=rsl[:, h, :], in_=a.v, func=AF.Silu)
                    yield

                run_interleaved([v_gen(0), gate_gen(0), gate_gen(1), v_gen(1), r_gen(0), v_gen(2), gate_gen(2), gate_gen(3), r_gen(1),
                                 v_gen(3), r_gen(2), r_gen(3)], 3)
                utb = ut32.un(1).bc([128, 4, 128])
                eglv = egl.v.re("p (h t) -> p h t", t=4)
                s3 = Sgla[l].v.re("p (h d) -> p h d", h=4)

                def r4(a_):
                    return a_.v.re("p (h t) -> p h t", h=4)

                def pre(tt):
                    ts = slice(tt * 128, (tt + 1) * 128)
                    sc = acc()
                    for h in range(4):
                        S.I("pe", "matmul", out=sc[:, h * 128:(h + 1) * 128], lhsT=kg[:, h, ts], rhs=qg[:, h, ts], start=True, stop=True)
                    S.I("dve", "tensor_tensor", out=sTm[tt % 2].v, in0=r4(sc), in1=utb, op=ALU.mult)
                    S.I("pool", "tensor_tensor", out=kkT[tt % 2].v, in0=kg[:, :, ts], in1=eglv[:, :, tt].un(2).bc([128, 4, 128]), op=ALU.mult)
                    tb = tbb()
                    for h in range(4):
                        S.I("pe", "transpose", out=tb[:, h * 128:(h + 1) * 128], in_=kkT[tt % 2][:, h, :], identity=ident_bf)
                    S.I("act", "activation", out=kkt[tt % 2].v, in_=tb.v.re("p (h t) -> p h t", h=4), func=AF.Copy)

                def post(tt):
                    ts = slice(tt * 128, (tt + 1) * 128)
                    oa = acc()
                    for h in range(4):
                        hs = slice(h * 128, (h + 1) * 128)
                        S.I("pe", "matmul", out=oa[:, hs], lhsT=vtok[tt][:, hs], rhs=sTm[tt % 2][:, h, :], start=True, stop=False)
                        S.I("pe", "matmul", out=oa[:, hs], lhsT=Sgla_b[l][:, hs], rhs=qg[:, h, ts], start=False, stop=True)
                    pm = acc()
                    for h in range(4):
                        hs = slice(h * 128, (h + 1) * 128)
                        S.I("pe", "matmul", out=pm[:, hs], lhsT=kkt[tt % 2][:, h, :], rhs=vtok[tt][:, hs], start=True, stop=True)
                    S.I("dve", "tensor_tensor", out=s3, in0=s3, in1=eglv[:, :, tt].un(2).bc([128, 4, 128]), op=ALU.mult)
                    S.I("dve", "tensor_tensor", out=Sgla[l].v, in0=pm.v, in1=Sgla[l].v, op=ALU.add)
                    S.I("act", "activation", out=Sgla_b[l].v, in_=Sgla[l].v, func=AF.Copy)
                    headnorm(oa, rsl[:, :, ts], par[:, P_GLANG + l:P_GLANG + l + 1], ogla[:, :, ts])

                pre(0)
                for tt in range(4):
                    if tt + 1 < 4:
                        pre(tt + 1)
                    post(tt)
                if st == 0:
                    dump("ogla%d" % l, ogla.v, [128, 4, T])

        def att_phase(l, st):
            with ExitStack() as ph:
                qT = sb(ph, [128, 4, T], BF16)
                tmp = [sb(ph, [128, 640], F32) for _ in range(2)]
                pT = [sb(ph, [128, 640], BF16) for _ in range(2)]
                otok = sb(ph, [128, 4, 512], BF16)
                rden = sb(ph, [128, 4], F32)
                slot = st % 2
                wq = load_in(l, C_AQ, 512)
                for c in range(4):
                    a = acc()
                    proj_fm(a, wq[:, :, c * 128:(c + 1) * 128])
                    S.I("act", "activation", out=qT[:, c, :], in_=a.v, func=AF.Copy, scale=0.125)
                wk = load_in(l, C_AK, 512)
                for c in range(4):
                    a = acc()
                    proj_fm(a, wk[:, :, c * 128:(c + 1) * 128])
                    S.I("dve", "tensor_copy", out=kTh[l][:, c, slot * T:(slot + 1) * T], in_=a.v)
                wv = load_in(l, C_AV, 512)
                for tt in range(4):
                    m = st * 4 + tt
                    a = acc()
                    proj_tm(a, tt, wv, 512)
                    S.I("act", "activation", out=vh[l][:, m % 8, :].re("p (h e) -> p h e", e=65)[:, :, 0:64],
                        in_=a.v.re("p (h e) -> p h e", e=64), func=AF.Copy)
                S.fresh(SC)
                acc_state["list"] = [4, 5]
                pvs = {}

                def qk_exp(h, tt):
                    c, pb = h // 2, (h % 2) * 64
                    bmr = bmh[h % 2]
                    if tt == 0:
                        S.dma("sp", out=bmr.v, in_=bm_d[l][:, h * 640:(h + 1) * 640])
                    m = st * 4 + tt
                    kt0 = max(0, 4 - m)
                    ts = slice(tt * 128, (tt + 1) * 128)
                    sc = SC[tt % 2]
                    for kt in range(kt0, 5):
                        mk = m - 4 + kt
                        off = ((mk // 4) % 2) * T + (mk % 4) * 128
                        S.I("pe", "matmul", out=sc[:, kt * 128:(kt + 1) * 128], lhsT=kTh[l][pb:pb + 64, c, off:off + 128],
                            rhs=qT[pb:pb + 64, c, ts], start=True, stop=True)
                    cs = slice(kt0 * 128, 640)
                    tm, pt = tmp[tt % 2], pT[tt % 2]
                    S.I("dve", "tensor_tensor", out=tm[:, cs], in0=sc[:, cs], in1=bmr[:, cs], op=ALU.add)
                    S.I("act", "activation", out=pt[:, cs], in_=tm[:, cs], func=AF.Exp)

                def pv_part(h, tt):
                    if tt == 0:
                        pvs[h] = acc()
                    pv = pvs[h]
                    m = st * 4 + tt
                    kt0 = max(0, 4 - m)
                    pt = pT[tt % 2]
                    for kt in range(kt0, 5):
                        mk = m - 4 + kt
                        S.I("pe", "matmul", out=pv[:, tt * 65:(tt + 1) * 65], lhsT=pt[:, kt * 128:(kt + 1) * 128],
                            rhs=vh[l][:, mk % 8, h * 65:(h + 1) * 65], start=(kt == kt0), stop=(kt == 4))
                    if tt == 3:
                        pv3 = pv[:, 0:260].re("p (t e) -> p t e", e=65)
                        S.I("dve", "reciprocal", out=rden.v, in_=pv3[:, :, 64])
                        S.I("dve", "tensor_tensor", out=otok[:, :, h * 64:(h + 1) * 64], in0=pv3[:, :, 0:64],
                            in1=rden.v.un(2).bc([128, 4, 64]), op=ALU.mult)

                its = [(h, tt) for h in range(8) for tt in range(4)]
                qk_exp(*its[0])
                for i in range(len(its)):
                    if i + 1 < len(its):
                        qk_exp(*its[i + 1])
                    pv_part(*its[i])
                for tt in range(4):
                    ts = slice(tt * 128, (tt + 1) * 128)
                    tb = tbb()
                    for c in range(4):
                        S.I("pe", "transpose", out=tb[:, c * 128:(c + 1) * 128], in_=otok[:, tt, c * 128:(c + 1) * 128], identity=ident_bf)
                    S.I("act", "activation", out=oatt[:, :, ts], in_=tb.v.re("p (c t) -> p c t", c=4), func=AF.Copy)
                acc_state["list"] = list(range(6))
                S.fresh(ACC[0:4])
                if st == 0:
                    dump("oatt%d" % l, oatt.v, [128, 4, T])

        def gdn_phase(l, st):
            with ExitStack() as ph:
                qn = [sb(ph, [128, T], BF16) for _ in range(4)]
                kn = [sb(ph, [128, T], BF16) for _ in range(4)]
                vT = [sb(ph, [128, T], BF16) for _ in range(4)]
                zs = sb(ph, [128, 4, T], BF16)
                cv = ExitStack()
                NSL = 3
                cin = [sb(cv, [128, T + 4], BF16) for _ in range(NSL)]
                cacc = [sb(cv, [128, T], F32) for _ in range(NSL)]
                csq = [sb(cv, [128, T], BF16) for _ in range(NSL)]
                crs = [sb(cv, [128, T], F32) for _ in range(NSL)]
                cdg = [sb(cv, [128, 4, 128], BF16) for _ in range(NSL)]
                cw = par[:, P_CONV + l * 48:P_CONV + (l + 1) * 48]
                wts = {}

                def get_w(kind):
                    if kind not in wts:
                        wts[kind] = load_in(l, {"q": C_DQ, "k": C_DK, "v": C_DV, "z": C_DZ}[kind], 512)
                    return wts[kind]

                def conv_gen(ci, slot):
                    kind = "qkv"[ci // 4]
                    h = ci % 4
                    wt = get_w(kind)
                    a = acc()
                    proj_fm(a, wt[:, :, h * 128:(h + 1) * 128])
                    x_ = cin[slot]
                    S.I("act", "activation", out=x_[:, 4:T + 4], in_=a.v, func=AF.Copy)
                    S.I("pool", "tensor_copy", out=x_[:, 0:4], in_=chist[l][:, ci, :])
                    dg = cdg[slot]
                    S.I("pool", "tensor_tensor", out=dg.v, in0=ident_bf.un(1).bc([128, 4, 128]),
                        in1=cw[:, ci * 4:ci * 4 + 4].un(2).bc([128, 4, 128]), op=ALU.mult)
                    yield
                    o = cacc[slot]
                    a = acc()
                    for i in range(4):
                        S.I("pe", "matmul", out=a.v, lhsT=dg[:, i, :], rhs=x_[:, 1 + i:T + 1 + i], start=(i == 0), stop=(i == 3))
                    S.I("pool", "tensor_copy", out=chist[l][:, ci, :], in_=x_[:, T:T + 4])
                    if kind == "v":
                        S.I("act", "activation", out=vT[h].v, in_=a.v, func=AF.Silu)
                        return
                    S.I("act", "activation", out=o.v, in_=a.v, func=AF.Silu)
                    yield
                    q_, r_ = csq[slot], crs[slot]
                    a = acc()
                    S.I("act", "activation", out=q_.v, in_=o.v, func=AF.Square)
                    S.I("pe", "matmul", out=a.v, lhsT=ones_bf, rhs=q_.v, start=True, stop=True)
                    S.I("act", "activation", out=r_.v, in_=a.v, func=AF.Sqrt, scale=1.0, bias=EPS)
                    yield
                    S.I("dve", "reciprocal", out=r_.v, in_=r_.v)
                    dst = qn[h].v if kind == "q" else kn[h].v
                    S.I("dve", "scalar_tensor_tensor", out=dst, in0=o.v, scalar=(128 ** -0.5 if kind == "q" else 1.0), in1=r_.v,
                        op0=ALU.mult, op1=ALU.add if False else ALU.mult)

                def z_gen(h):
                    wz = get_w("z")
                    a = acc()
                    proj_fm(a, wz[:, :, h * 128:(h + 1) * 128])
                    S.I("act", "activation", out=zs[:, h, :], in_=a.v, func=AF.Silu)
                    yield

                slots = list(range(NSL))

                def slotted(ci):
                    sl = slots.pop(0)
                    yield from conv_gen(ci, sl)
                    slots.append(sl)

                run_interleaved([slotted(ci) for ci in range(12)] + [z_gen(h) for h in range(4)], NSL)
                wab = load_in(l, C_AB, 8)
                cv.close()

                def mkbufs():
                    Bf = {}
                    Bf["sc_"] = sb(ph, [128, 64], F32)
                    for nm in ("M1", "M1b"):
                        Bf[nm] = sb(ph, [128, 4, 128], F32)
                    for nm in ("ET", "ETb", "Bm", "Am", "Y"):
                        Bf[nm] = sb(ph, [128, 4, 128], NEU_DT)
                    for nm in ("Yb", "attnT", "kbg", "kkt", "vbt", "qgT", "nwT", "vnew"):
                        Bf[nm] = sb(ph, [128, 4, 128], BF16)
                    return Bf

                bufsets = [mkbufs(), mkbufs()]
                idb = ident32.un(1).bc([128, 4, 128])
                utb = ut32.un(1).bc([128, 4, 128])
                turn = [0]

                def b4(v):
                    return v.un(2).bc([128, 4, 128])

                def r4(a):
                    return a.v.re("p (h t) -> p h t", h=4)

                def gdn_tile(tt, Bf):
                    own = [0, 1, 2] if tt % 2 == 0 else [3, 4, 5]
                    ctr = [0]

                    def A():
                        r = ACC[own[ctr[0] % 3]]
                        ctr[0] += 1
                        return r

                    sc_, M1, M1b, ET, ETb, Bm, Am, Y = (Bf[k] for k in ("sc_", "M1", "M1b", "ET", "ETb", "Bm", "Am", "Y"))
                    Yb, attnT, kbg, kkt, vbt, qgT, nwT, vnew = (Bf[k] for k in ("Yb", "attnT", "kbg", "kkt", "vbt", "qgT", "nwT", "vnew"))
                    EGR, P2, Q2 = M1b, ET, ETb
                    ts = slice(tt * 128, (tt + 1) * 128)
                    a = msm()
                    proj_tm(a, tt, wab, 8)
                    la, lb, gc, gl = sc_[:, 0:4], sc_[:, 4:8], sc_[:, 8:12], sc_[:, 12:16]
                    S.I("dve", "tensor_tensor", out=sc_[:, 16:20], in0=a[:, 0:4], in1=par[:, P_DTB + l * 4:P_DTB + l * 4 + 4], op=ALU.add)
                    S.I("act", "activation", out=sc_[:, 20:24], in_=a[:, 4:8], func=AF.Exp, scale=-1.0)
                    yield
                    S.I("act", "activation", out=sc_[:, 16:20], in_=sc_[:, 16:20], func=AF.Exp)
                    S.I("act", "activation", out=sc_[:, 16:20], in_=sc_[:, 16:20], func=AF.Ln, bias=1.0)
                    S.I("act", "activation", out=lb, in_=sc_[:, 20:24], func=AF.Ln, bias=1.0)
                    yield
                    S.I("dve", "tensor_tensor", out=la, in0=sc_[:, 16:20], in1=der[:, 8 + l * 4:12 + l * 4], op=ALU.mult)
                    S.I("dve", "tensor_scalar", out=sc_[:, 36:40], in0=lb, scalar1=-1.0, scalar2=None, op0=ALU.mult)
                    nlb = sc_[:, 36:40]
                    S.I("dve", "tensor_tensor", out=M1.v, in0=utb, in1=b4(la), op=ALU.mult)
                    S.I("pool", "tensor_tensor", out=M1b.v, in0=idb, in1=b4(nlb), op=ALU.mult)
                    S.I("pool", "tensor_tensor", out=M1b.v, in0=M1b.v, in1=M1.v, op=ALU.add)
                    yield
                    g2 = msm()
                    S.I("pe", "matmul", out=g2[:, 0:4], lhsT=ut32, rhs=la, start=True, stop=True)
                    S.I("pe", "matmul", out=g2[:, 4:8], lhsT=ones32, rhs=la, start=True, stop=True)
                    S.I("dve", "tensor_copy", out=sc_[:, 8:16], in_=g2[:, 0:8])
                    m1f = M1.v.re("p h t -> p (h t)")
                    m1bf = M1b.v.re("p h t -> p (h t)")
                    d1, d2, gr = A(), A(), A()
                    S.I("pe", "matmul", out=d1.v, lhsT=ones32, rhs=m1f, start=True, stop=False)
                    S.I("pe", "matmul", out=d1.v, lhsT=ident32, rhs=negi4, start=False, stop=True)
                    S.I("pe", "matmul", out=d2.v, lhsT=ones32, rhs=m1bf, start=True, stop=False)
                    S.I("pe", "matmul", out=d2.v, lhsT=ident32, rhs=negs4, start=False, stop=True)
                    S.I("pe", "matmul", out=gr.v, lhsT=ones32, rhs=m1f, start=True, stop=True)
                    yield
                    S.I("dve", "tensor_tensor", out=sc_[:, 24:28], in0=gc, in1=lb, op=ALU.subtract)
                    S.I("dve", "tensor_tensor", out=sc_[:, 28:32], in0=gl, in1=gc, op=ALU.subtract)
                    S.I("dve", "tensor_copy", out=sc_[:, 32:36], in_=gl)
                    S.I("act", "activation", out=sc_[:, 40:56], in_=sc_[:, 24:40], func=AF.Exp)
                    ckbg, ckk, egl, beta = sc_[:, 40:44], sc_[:, 44:48], sc_[:, 48:52], sc_[:, 52:56]
                    S.I("dve", "tensor_tensor", out=ET.v, in0=r4(d1), in1=b4(gc), op=ALU.subtract)
                    S.I("act", "activation", out=ET.v, in_=ET.v, func=AF.Exp)
                    yield
                    S.I("dve", "tensor_tensor", out=ETb.v, in0=r4(d2), in1=b4(gc), op=ALU.subtract)
                    S.I("act", "activation", out=ETb.v, in_=ETb.v, func=AF.Exp)
                    S.I("act", "activation", out=EGR.v, in_=r4(gr), func=AF.Exp)
                    yield
                    kk_, qk_ = A(), A()
                    for h in range(4):
                        hs = slice(h * 128, (h + 1) * 128)
                        S.I("pe", "matmul", out=kk_[:, hs], lhsT=kn[h][:, ts], rhs=kn[h][:, ts], start=True, stop=True)
                        S.I("pe", "matmul", out=qk_[:, hs], lhsT=kn[h][:, ts], rhs=qn[h][:, ts], start=True, stop=True)
                    yield
                    S.I("dve", "tensor_tensor", out=Bm.v, in0=r4(kk_), in1=ETb.v, op=ALU.mult)
                    S.I("dve", "tensor_tensor", out=attnT.v, in0=r4(qk_), in1=ET.v, op=ALU.mult)
                    at = A()
                    for h in range(4):
                        S.I("pe", "transpose", out=at[:, h * 128:(h + 1) * 128], in_=V(Bm, Bm.ap[:, h, :].bitcast(F32)), identity=ident32)
                    yield
                    S.I("act", "activation", out=Am.v, in_=r4(at), func=AF.Copy)
                    S.I("pool", "tensor_tensor", out=Y.v, in0=idb, in1=Bm.v, op=ALU.subtract)
                    for h in range(4):
                        S.I("pool", "tensor_tensor", out=qgT[:, h, :], in0=qn[h][:, ts], in1=EGR[:, h, :], op=ALU.mult)
                    yield
                    Pc, Qc, Pn, Qn = Am, Bm, P2, Q2
                    for k in range(1, 7):
                        pp = A()
                        for h in range(4):
                            S.I("pe", "matmul", out=pp[:, h * 128:(h + 1) * 128], lhsT=Qc[:, h, :], rhs=Pc[:, h, :], start=True, stop=True)
                        qq = A()
                        if k < 6:
                            for h in range(4):
                                S.I("pe", "matmul", out=qq[:, h * 128:(h + 1) * 128], lhsT=Pc[:, h, :], rhs=Qc[:, h, :], start=True, stop=True)
                        yield
                        S.I("act", "activation", out=Pn.v, in_=r4(pp), func=AF.Copy)
                        if k < 6:
                            S.I("dve", "tensor_copy", out=Qn.v, in_=r4(qq))
                        yy = A()
                        for h in range(4):
                            S.I("pe", "matmul", out=yy[:, h * 128:(h + 1) * 128], lhsT=Pn[:, h, :], rhs=Y[:, h, :], start=True, stop=True)
                        yield
                        S.I("dve", "tensor_tensor", out=Y.v, in0=r4(yy), in1=Y.v, op=ALU.add)
                        Pc, Pn = Pn, Pc
                        Qc, Qn = Qn, Qc
                    S.I("act", "activation", out=Yb.v, in_=Y.v, func=AF.Copy)
                    yield
                    tk, tv = tbb(), tbb()
                    for h in range(4):
                        S.I("pe", "transpose", out=tk[:, h * 128:(h + 1) * 128], in_=kn[h][:, ts], identity=ident_bf)
                        S.I("pe", "transpose", out=tv[:, h * 128:(h + 1) * 128], in_=vT[h][:, ts], identity=ident_bf)
                    S.I("dve", "tensor_tensor", out=kbg.v, in0=r4(tk), in1=b4(ckbg), op=ALU.mult)
                    S.I("dve", "tensor_tensor", out=kkt.v, in0=r4(tk), in1=b4(ckk), op=ALU.mult)
                    S.I("dve", "tensor_tensor", out=vbt.v, in0=r4(tv), in1=b4(beta), op=ALU.mult)
                    yield
                    ww = A()
                    for h in range(4):
                        S.I("pe", "matmul", out=ww[:, h * 128:(h + 1) * 128], lhsT=kbg[:, h, :], rhs=Yb[:, h, :], start=True, stop=True)
                    yield
                    S.I("act", "activation", out=nwT.v, in_=r4(ww), func=AF.Copy, scale=-1.0)
                    while turn[0] != tt:
                        yield
                    vn = A()
                    for h in range(4):
                        hs = slice(h * 128, (h + 1) * 128)
                        S.I("pe", "matmul", out=vn[:, hs], lhsT=Yb[:, h, :], rhs=vbt[:, h, :], start=True, stop=False)
                        S.I("pe", "matmul", out=vn[:, hs], lhsT=nwT[:, h, :], rhs=Sgdn_b[l][:, hs], start=False, stop=True)
                    S.I("act", "activation", out=vnew.v, in_=r4(vn), func=AF.Copy)
                    oo = A()
                    for h in range(4):
                        hs = slice(h * 128, (h + 1) * 128)
                        S.I("pe", "matmul", out=oo[:, hs], lhsT=Sgdn_b[l][:, hs], rhs=qgT[:, h, :], start=True, stop=False)
                        S.I("pe", "matmul", out=oo[:, hs], lhsT=vnew[:, h, :], rhs=attnT[:, h, :], start=False, stop=True)
                    pst = A()
                    for h in range(4):
                        hs = slice(h * 128, (h + 1) * 128)
                        S.I("pe", "matmul", out=pst[:, hs], lhsT=kkt[:, h, :], rhs=vnew[:, h, :], start=True, stop=True)
                    s3 = Sgdn[l].v.re("p (h t) -> p h t", h=4)
                    S.I("dve", "tensor_tensor", out=s3, in0=s3, in1=b4(egl), op=ALU.mult)
                    S.I("dve", "tensor_tensor", out=Sgdn[l].v, in0=pst.v, in1=Sgdn[l].v, op=ALU.add)
                    S.I("act", "activation", out=Sgdn_b[l].v, in_=Sgdn[l].v, func=AF.Copy)
                    turn[0] += 1
                    yield
                    headnorm(oo, zs[:, :, ts], par[:, P_GDNNG + l:P_GDNNG + l + 1], ogdn[:, :, ts], accfn=A)

                run_interleaved([gdn_tile(tt, bufsets[tt % 2]) for tt in range(4)], 2, lead=GDN_LEAD)
                if st == 0:
                    dump("ogdn%d" % l, ogdn.v, [128, 4, T])

        def merge_phase(l, st):
            with ExitStack() as ph:
                gates = sb(ph, [128, 24, T], BF16)
                y = [sb(ph, [128, T], BF16) for _ in range(KC)]
                tg = [sb(ph, [128, T], F32) for _ in range(3)]
                for j in range(6):
                    wg = load_in(l, C_MG + j * 512, 512)
                    for cc in range(4):
                        a = acc()
                        proj_fm(a, wg[:, :, cc * 128:(cc + 1) * 128])
                        S.I("act", "activation", out=gates[:, j * 4 + cc, :], in_=a.v, func=AF.Sigmoid)
                wbr = []
                for name, wd in (("bg", w_bg_b), ("ba", w_ba_b), ("bd", w_bd_b)):
                    wbr.append(load(name, l, wd[l].rearrange("(c p) d -> p c d", p=128), 4, D))
                srcs = [[ogla[:, h, :] for h in range(4)], [oatt[:, c, :] for c in range(4)], [ogdn[:, h, :] for h in range(4)]]
                for dc in range(KC):
                    ds = slice(dc * 128, (dc + 1) * 128)
                    for b in range(3):
                        a = acc()
                        for c in range(4):
                            S.I("pe", "matmul", out=a.v, lhsT=wbr[b][:, c, ds], rhs=srcs[b][c], start=(c == 0), stop=(c == 3))
                        S.I("dve", "tensor_tensor", out=tg[b].v, in0=a.v, in1=gates[:, b * 8 + dc, :], op=ALU.mult)
                    S.I("pool", "tensor_tensor", out=tg[0].v, in0=tg[0].v, in1=tg[1].v, op=ALU.add)
                    S.I("pool", "tensor_tensor", out=y[dc].v, in0=tg[0].v, in1=tg[2].v, op=ALU.add)
                if st == 0:
                    for dc in range(KC):
                        dump("y%d" % l, y[dc].v, [128, T])
                for half in range(2):
                    wo = load("out", l, w_out_b[l].rearrange("(c p) d -> p c d", p=128)[:, :, half * 512:(half + 1) * 512], KC, 512)
                    for dd in range(4):
                        dc2 = half * 4 + dd
                        a = acc()
                        for dc in range(KC):
                            S.I("pe", "matmul", out=a.v, lhsT=wo[:, dc, dd * 128:(dd + 1) * 128], rhs=y[dc].v, start=(dc == 0), stop=(dc == KC - 1))
                        S.I("dve", "tensor_tensor", out=xT[dc2].v, in0=a.v, in1=xT[dc2].v, op=ALU.add)

        def ffn_phase(l, st):
            with ExitStack() as ph:
                actb = [sb(ph, [128, T], BF16) for _ in range(22)]
                sg = [sb(ph, [128, T], F32) for _ in range(2)]
                rmsnorm_apply(P_FFNG + l * 8)
                for j in range(6):
                    n = 512 if j < 5 else 256
                    wg = load("fi", l, w_fi_b[l].rearrange("(k p) c -> p k c", p=128)[:, :, j * 512:j * 512 + n], KC, n)
                    wu = load("fi", l, w_fi_b[l].rearrange("(k p) c -> p k c", p=128)[:, :, FH + j * 512:FH + j * 512 + n], KC, n)
                    for cc in range(n // 128):
                        c = j * 4 + cc
                        ag = acc()
                        proj_fm(ag, wg[:, :, cc * 128:(cc + 1) * 128])
                        au = acc()
                        proj_fm(au, wu[:, :, cc * 128:(cc + 1) * 128])
                        s_ = sg[c % 2]
                        S.I("act", "activation", out=s_.v, in_=ag.v, func=AF.Silu)
                        S.I("dve", "tensor_tensor", out=actb[c].v, in0=au.v, in1=s_.v, op=ALU.mult)
                fov = w_fo_b[l].rearrange("(c p) d -> p c d", p=128)
                cgrp = [(0, 8), (8, 8), (16, 6)]
                for half in range(2):
                    wo = [load("fo", l, fov[:, c0:c0 + n, half * 512:(half + 1) * 512], n, 512) for c0, n in cgrp]
                    for dd in range(4):
                        dc2 = half * 4 + dd
                        a = acc()
                        for c in range(22):
                            S.I("pe", "matmul", out=a.v, lhsT=wo[c // 8][:, c % 8, dd * 128:(dd + 1) * 128], rhs=actb[c].v, start=(c == 0), stop=(c == 21))
                        S.I("dve", "tensor_tensor", out=xT[dc2].v, in0=a.v, in1=xT[dc2].v, op=ALU.add)

        def run_interleaved(gens, width, lead=0):
            it = iter(gens)
            active = []
            if lead:
                g0 = next(it)
                active.append(g0)
                for _ in range(lead):
                    next(g0)
            while True:
                while len(active) < width:
                    g = next(it, None)
                    if g is None:
                        break
                    active.append(g)
                if not active:
                    break
                for g in list(active):
                    try:
                        next(g)
                    except StopIteration:
                        active.remove(g)

        def rmsnorm_apply(gc0):
            rmsnorm(gc0)
            for kc in range(KC):
                S.I("dve", "scalar_tensor_tensor", out=hT[kc].v, in0=xT[kc].v,
                    scalar=par[:, gc0 + kc:gc0 + kc + 1], in1=rs.v, op0=ALU.mult, op1=ALU.mult)

        from contextlib import contextmanager

        @contextmanager
        def phase_mark(name):
            marks.append((name, "b", dict(S.seq)))
            yield
            marks.append((name, "e", dict(S.seq)))

        for st in range(NST if upto > 0 else 0):
            io_ph = ExitStack()
            xin = [sb(io_ph, [128, D], F32) for _ in range(2)]
            for tt in range(4):
                xi = xin[tt % 2]
                r0 = (st * 4 + tt) * 128
                S.dma("sp", out=xi.v, in_=x_d[r0:r0 + 128, :])
                for g in range(2):
                    a = acc()
                    for j in range(4):
                        kc = g * 4 + j
                        S.I("pe", "transpose", out=a[:, j * 128:(j + 1) * 128], in_=xi[:, kc * 128:(kc + 1) * 128], identity=ident32)
                    for j in range(4):
                        kc = g * 4 + j
                        S.I("act" if j % 2 == 0 else "dve", "activation" if j % 2 == 0 else "tensor_copy",
                            out=xT[kc][:, tt * 128:(tt + 1) * 128], in_=a[:, j * 128:(j + 1) * 128],
                            **({"func": AF.Copy} if j % 2 == 0 else {}))
            io_ph.close()
            for l in range(depth):
                if upto < 0.6:
                    break
                if upto < 0.8:
                    rmsnorm(P_MIXG + l * 8)
                    break
                rmsnorm_apply(P_MIXG + l * 8)
                if st == 0:
                    for kc in range(KC):
                        dump("h%d" % l, hT[kc].v, [128, T])
                if upto < 2:
                    break
                with phase_mark("gla"):
                    gla_phase(l, st)
                if upto < 3:
                    break
                with phase_mark("att"):
                    att_phase(l, st)
                if upto < 4:
                    break
                with phase_mark("gdn"):
                    gdn_phase(l, st)
                if upto < 5:
                    break
                with phase_mark("merge"):
                    merge_phase(l, st)
                if upto < 6:
                    break
                if st == 0:
                    for kc in range(KC):
                        dump("xmid%d" % l, xT[kc].v, [128, T])
                with phase_mark("ffn"):
                    ffn_phase(l, st)
                if st == 0:
                    for kc in range(KC):
                        dump("x%d" % l, xT[kc].v, [128, T])
            if upto < 7:
                continue
            io_ph = ExitStack()
            xin = [sb(io_ph, [128, D], F32) for _ in range(2)]
            rmsnorm(P_FING)
            for kc in range(KC):
                S.I("dve", "scalar_tensor_tensor", out=xT[kc].v, in0=xT[kc].v,
                    scalar=par[:, P_FING + kc:P_FING + kc + 1], in1=rs.v, op0=ALU.mult, op1=ALU.mult)
            for tt in range(4):
                xo = xin[tt % 2]
                r0 = (st * 4 + tt) * 128
                for g in range(2):
                    a = acc()
                    for j in range(4):
                        kc = g * 4 + j
                        S.I("pe", "transpose", out=a[:, j * 128:(j + 1) * 128], in_=xT[kc][:, tt * 128:(tt + 1) * 128], identity=ident32)
                    S.I("act" if g == 0 else "dve", "activation" if g == 0 else "tensor_copy",
                        out=xo[:, g * 512:(g + 1) * 512], in_=a.v, **({"func": AF.Copy} if g == 0 else {}))
                S.dma("sp", out=out_d[r0:r0 + 128, :], in_=xo.v)
            io_ph.close()
        S.finish()
        print("instructions:", S.ninstr, "waits:", S.nwait, "incs:", S.ninc, flush=True)
    return nc, dbg_out


def _host_consts():
    c = np.zeros((128, CW), np.float32)
    j = np.arange(128)[:, None]
    i = np.arange(128)[None, :]
    c[:, K_ID:K_ID + 128] = (j == i)
    ut = (j <= i).astype(np.float32)
    c[:, K_UT:K_UT + 128] = ut
    ni = np.where(j <= i, 0.0, NEG).astype(np.float32)
    ns = np.where(j < i, 0.0, NEG).astype(np.float32)
    c[:, K_NI4:K_NI4 + 512] = np.tile(ni, (1, 4))
    c[:, K_NS4:K_NS4 + 512] = np.tile(ns, (1, 4))
    rm = np.ones((128, 512), np.float32)
    rm[:, 0::128] = 0.0
    c[:, K_RM:K_RM + 512] = rm
    c[:, K_ONE:K_ONE + 128] = 1.0
    return c


def _host_params(inp, depth):
    p = np.zeros((128, PW), np.float32)

    def cols(v):
        return np.ascontiguousarray(v.reshape(-1, 128).T)

    for l in range(depth):
        p[:, P_MIXG + l * 8:P_MIXG + (l + 1) * 8] = cols(inp["mix_norm_g"][l])
        p[:, P_FFNG + l * 8:P_FFNG + (l + 1) * 8] = cols(inp["ffn_norm_g"][l])
        p[:, P_GLAB + l * 4:P_GLAB + (l + 1) * 4] = cols(inp["gla_b_gate"][l])
        p[:, P_GLANG + l] = inp["gla_norm_g"][l]
        p[:, P_GDNNG + l] = inp["gdn_norm_g"][l]
        cw = inp["gdn_conv_w"][l]
        p[:, P_CONV + l * 48:P_CONV + (l + 1) * 48] = cw.reshape(4, 12, 128).transpose(2, 1, 0).reshape(128, 48)
        p[:, P_ALOG + l * 4:P_ALOG + (l + 1) * 4] = np.broadcast_to(inp["gdn_a_log"][l][None, :], (128, 4))
        p[:, P_DTB + l * 4:P_DTB + (l + 1) * 4] = np.broadcast_to(inp["gdn_dt_bias"][l][None, :], (128, 4))
    p[:, P_FING:P_FING + 8] = cols(inp["final_norm_g"])
    return p


def _host_bias(rel_bias, depth):
    j = np.arange(128)[:, None, None]
    kt = np.arange(5)[None, :, None]
    i = np.arange(128)[None, None, :]
    rel = (4 - kt) * 128 + i - j
    idx = np.clip(rel, -63, 256) + 63
    dchunk = 2 * (kt - 4) + (j >= 64).astype(np.int64) - (i >= 64).astype(np.int64)
    valid = (dchunk >= -8) & (dchunk <= 0)
    idx = np.where(valid, idx, 320)
    out = np.zeros((depth, 128, 8, 640), np.float32)
    for l in range(depth):
        ext = np.concatenate([rel_bias[l], np.full((8, 1), NEG, np.float32)], axis=1)
        g = ext[:, idx]
        out[l] = g.transpose(1, 0, 2, 3).reshape(128, 8, 640)
    return out.reshape(depth, 128, 8 * 640)


def prepare_common(inp, depth=DEPTH):
    f = lambda a: np.ascontiguousarray(np.asarray(a, dtype=np.float32))
    com = {
        "w_in": f(inp["w_in"])[:depth], "w_branch_gla": f(inp["w_branch_gla"])[:depth],
        "w_branch_att": f(inp["w_branch_att"])[:depth], "w_branch_gdn": f(inp["w_branch_gdn"])[:depth],
        "w_out": f(inp["w_out"])[:depth], "w_ffn_in": f(inp["w_ffn_in"])[:depth], "w_ffn_out": f(inp["w_ffn_out"])[:depth],
        "wgu": np.ascontiguousarray(f(inp["gla_w_gate_up"])[:depth].transpose(1, 0, 2).reshape(16, depth * 512)),
        "cst": _host_consts(),
        "par": _host_params({k: f(v) for k, v in inp.items()}, depth),
        "bm": _host_bias(f(inp["att_rel_bias"]), depth),
    }
    return com


_CACHE = {}


def kernel(**inputs):
    x = np.asarray(inputs["x"], dtype=np.float32)
    B, SEQ, _ = x.shape
    NST = SEQ // T
    if NST not in _CACHE:
        _CACHE[NST] = build(NST)[0]
    nc = _CACHE[NST]
    com = prepare_common(inputs)
    in_maps = []
    for b in range(B):
        m = dict(com)
        m["x"] = np.ascontiguousarray(x[b])
        in_maps.append(m)
    res = run_bass_kernel_spmd(nc, in_maps, core_ids=list(range(B)))
    return np.stack([np.asarray(r["out"]) for r in res.results], axis=0).astype(np.float32)
```

```python
import numpy as np
from contextlib import ExitStack
import concourse.bass as bass
import concourse.mybir as mybir
from concourse.bass_utils import run_bass_kernel_spmd

F32 = mybir.dt.float32
BF16 = mybir.dt.bfloat16
AF = mybir.ActivationFunctionType
ALU = mybir.AluOpType

D = 1024
T = 512
KC = 8
INW = 8728
FH = 2816
DEPTH = 2
C_GQ, C_GK, C_GV, C_GR, C_LR = 0, 512, 1024, 1536, 2048
C_AQ, C_AK, C_AV = 2064, 2576, 3088
C_DQ, C_DK, C_DV, C_DZ, C_AB, C_MG = 3600, 4112, 4624, 5136, 5648, 5656
NEG = -30000.0
EPS = 1e-6
WST = 4096

K_ID, K_UT, K_NI4, K_NS4, K_RM, K_ONE = 0, 128, 256, 768, 1280, 1792
CW = 1920
P_MIXG, P_FFNG, P_FING, P_GLAB, P_GLANG, P_GDNNG, P_CONV, P_ALOG, P_DTB = 0, 16, 32, 40, 48, 50, 52, 148, 156
PW = 164


class Reg:
    __slots__ = ("ap", "w", "r", "excl")

    def __init__(self, ap, excl=False):
        self.ap = ap
        self.w = None
        self.r = {}
        self.excl = excl

    def __getitem__(self, idx):
        return V(self, self.ap[idx])

    @property
    def v(self):
        return V(self, self.ap)


class V:
    __slots__ = ("reg", "ap")

    def __init__(self, reg, ap):
        self.reg = reg
        self.ap = ap

    def __getitem__(self, idx):
        return V(self.reg, self.ap[idx])

    @property
    def v(self):
        return self

    def re(self, s, **kw):
        return V(self.reg, self.ap.rearrange(s, **kw))

    def bc(self, shape):
        return V(self.reg, self.ap.to_broadcast(shape))

    def un(self, axis):
        return V(self.reg, self.ap.unsqueeze(axis))


WKEYS = ("out", "accum_out", "ap")
import os as _os
EAGER_INC = _os.environ.get("EAGER_INC", "1") == "1"
GDN_LEAD = int(_os.environ.get("GDN_LEAD", "0"))
NEU_DT = mybir.dt.float32r if _os.environ.get("NEU_F32R", "0") == "1" else F32


class Sched:
    def __init__(self, nc, stack, ndsem=40):
        self.nc = nc
        self.stack = stack
        self.once = []
        self.engs = {"pe": nc.tensor, "act": nc.scalar, "dve": nc.vector, "pool": nc.gpsimd, "sp": nc.sync}
        self.sem = {k: stack.enter_context(nc.semaphore("s_" + k)) for k in self.engs}
        self.seq = {k: 0 for k in self.engs}
        self.last = {k: None for k in self.engs}
        self.incs = {k: [] for k in self.engs}
        self.waited = {k: {} for k in self.engs}
        self.dsems = [stack.enter_context(nc.semaphore("d%d" % i)) for i in range(ndsem)]
        self.dcnt = [0] * ndsem
        self.dnext = 0
        self.ninstr = 0
        self.nwait = 0
        self.ninc = 0

    def _resolve(self, tok):
        if tok[0] == "d":
            return tok[1], tok[2]
        e, q = tok[1], tok[2]
        lst = self.incs[e]
        import bisect
        i = bisect.bisect_left(lst, q)
        if i == len(lst):
            self.last[e].then_inc(self.sem[e], 1)
            lst.append(self.seq[e])
            self.ninc += 1
        return self.sem[e], i + 1

    def _wait(self, e, raw, other):
        best = {}
        for lst, is_raw in ((raw, True), (other, False)):
            for d in lst:
                if d is None:
                    continue
                if d[0] == "e" and d[1] == e and (e == "pe" or not is_raw):
                    continue
                k = ("d", id(d[1])) if d[0] == "d" else ("e", d[1])
                if k not in best or best[k][2] < d[2]:
                    best[k] = d
        eng = self.engs[e]
        for k, d in best.items():
            if d[0] == "e":
                if self.waited[e].get(k, 0) >= d[2]:
                    continue
                s, v = self._resolve(d)
                self.waited[e][k] = self.incs[d[1]][v - 1]
            else:
                if self.waited[e].get(k, 0) >= d[2]:
                    continue
                s, v = d[1], d[2]
                self.waited[e][k] = v
            eng.wait_ge(s, v)
            self.nwait += 1

    def _collect(self, reads, writes):
        raw = [b.w for b in reads]
        other = []
        for b in reads:
            if b.excl:
                other.extend(b.r.values())
        for b in writes:
            other.append(b.w)
            other.extend(b.r.values())
        return raw, other

    def _commit(self, key, tok, reads, writes):
        for b in reads:
            b.r[key] = tok
        for b in writes:
            b.w = tok
            b.r = {}

    def I(self, e, meth, xr=(), xw=(), **kw):
        reads, writes, args = list(xr), list(xw), {}
        for k, v in kw.items():
            if isinstance(v, V):
                (writes if k in WKEYS else reads).append(v.reg)
                args[k] = v.ap
            else:
                args[k] = v
        raw, other = self._collect(reads, writes)
        self._wait(e, raw, other)
        ins = getattr(self.engs[e], meth)(**args)
        self.seq[e] += 1
        self.last[e] = ins
        self.ninstr += 1
        if EAGER_INC and (e != "pe" or kw.get("stop", True)):
            ins.then_inc(self.sem[e], 1)
            self.incs[e].append(self.seq[e])
            self.ninc += 1
        self._commit(e, ("e", e, self.seq[e]), reads, writes)

    def dma(self, e, out, in_, xr=(), xw=()):
        reads, writes = list(xr), list(xw)
        if isinstance(out, V):
            if out.reg is not None:
                writes.append(out.reg)
            out = out.ap
        if isinstance(in_, V):
            if in_.reg is not None:
                reads.append(in_.reg)
            in_ = in_.ap
        i = self.dnext
        self.dnext = (self.dnext + 1) % len(self.dsems)
        s = self.dsems[i]
        raw, other = self._collect(reads, writes)
        other.append(("d", s, 16 * self.dcnt[i]))
        self._wait(e, raw, other)
        self.dcnt[i] += 1
        self.engs[e].dma_start(out=out, in_=in_).then_inc(s, 16)
        self.ninstr += 1
        self._commit("dma%d" % i, ("d", s, 16 * self.dcnt[i]), reads, writes)

    def dma_once(self, e, out, in_, xr=(), xw=()):
        reads, writes = list(xr), list(xw)
        if isinstance(out, V):
            if out.reg is not None:
                writes.append(out.reg)
            out = out.ap
        if isinstance(in_, V):
            if in_.reg is not None:
                reads.append(in_.reg)
            in_ = in_.ap
        s = self.stack.enter_context(self.nc.semaphore("o%d" % len(self.once)))
        self.once.append(s)
        raw, other = self._collect(reads, writes)
        self._wait(e, raw, other)
        self.engs[e].dma_start(out=out, in_=in_).then_inc(s, 16)
        self.ninstr += 1
        self._commit("once%d" % len(self.once), ("d", s, 16), reads, writes)

    def fresh(self, regs):
        snap = {e: ("e", e, self.seq[e]) for e in self.engs if self.seq[e] > 0}
        for i, sm in enumerate(self.dsems):
            if self.dcnt[i]:
                snap["dma%d" % i] = ("d", sm, 16 * self.dcnt[i])
        for i, sm in enumerate(self.once):
            snap["once%d" % (i + 1)] = ("d", sm, 16)
        for r in regs:
            r.w = None
            r.r = dict(snap)

    def finish(self):
        for i, s in enumerate(self.dsems):
            if self.dcnt[i]:
                self.nc.sync.wait_ge(s, 16 * self.dcnt[i])


def build(NST, depth=DEPTH, dbg=(), upto=99):
    nc = bass.Bass("TRN2", target_bir_lowering=False)
    NTOK = NST * T

    def din(name, shape, dt=F32):
        return nc.dram_tensor(name, list(shape), dt, kind="ExternalInput").ap()

    x_d = din("x", [NTOK, D])
    w_in_d = din("w_in", [depth, D, INW])
    w_bg_d = din("w_branch_gla", [depth, 512, D])
    w_ba_d = din("w_branch_att", [depth, 512, D])
    w_bd_d = din("w_branch_gdn", [depth, 512, D])
    w_out_d = din("w_out", [depth, D, D])
    w_fi_d = din("w_ffn_in", [depth, D, 2 * FH])
    w_fo_d = din("w_ffn_out", [depth, FH, D])
    wgu_d = din("wgu", [16, depth * 512])
    cst_d = din("cst", [128, CW])
    par_d = din("par", [128, PW])
    bm_d = din("bm", [depth, 128, 8 * 640])
    out_d = nc.dram_tensor("out", [NTOK, D], F32, kind="ExternalOutput").ap()

    def dscr(name, shape):
        return nc.dram_tensor(name, list(shape), BF16, kind="Internal").ap()

    w_in_b = dscr("w_in_b", [depth, D, INW])
    w_bg_b = dscr("w_bg_b", [depth, 512, D])
    w_ba_b = dscr("w_ba_b", [depth, 512, D])
    w_bd_b = dscr("w_bd_b", [depth, 512, D])
    w_out_b = dscr("w_out_b", [depth, D, D])
    w_fi_b = dscr("w_fi_b", [depth, D, 2 * FH])
    w_fo_b = dscr("w_fo_b", [depth, FH, D])

    dbg_out = {}
    marks = []
    build.marks = marks

    with ExitStack() as st0:
        S = Sched(nc, st0)
        ncount = [0]

        freed = []

        def _merge(dst, tok):
            if tok is None:
                return
            k = ("d", id(tok[1])) if tok[0] == "d" else ("e", tok[1])
            if k not in dst or dst[k][2] < tok[2]:
                dst[k] = tok

        def sb(stack, shape, dt, fresh=True):
            ncount[0] += 1
            t = stack.enter_context(nc.sbuf_tensor("t%d" % ncount[0], list(shape), dt))
            r = Reg(t[tuple(slice(None) for _ in shape)])
            if fresh:
                ml = nc.lookup_mloc(t)
                a0, a1 = ml.addr, ml.addr + ml.dims[1]
                deps = {}
                keep = []
                for f in freed:
                    if f[0] < a1 and a0 < f[1]:
                        for tok in f[2].values():
                            _merge(deps, tok)
                        if a0 <= f[0] and f[1] <= a1:
                            continue
                    keep.append(f)
                freed[:] = keep
                r.r = deps

                def retire(r=r, a0=a0, a1=a1):
                    d = {}
                    _merge(d, r.w)
                    for tok in r.r.values():
                        _merge(d, tok)
                    freed.append([a0, a1, d])

                stack.callback(retire)
            return r

        def gsb(shape, dt):
            return sb(st0, shape, dt, fresh=False)

        def dump(name, v, shape):
            if name not in dbg:
                return
            key = "dbg_%s_%d" % (name, len([k for k in dbg_out if k.startswith("dbg_" + name + "_")]))
            dt = v.ap.dtype
            d = nc.dram_tensor(key, list(shape), dt, kind="ExternalOutput").ap()
            dbg_out[key] = d
            S.dma("sp", out=d, in_=v)

        psA = st0.enter_context(nc.psum_tensor("psA", [128, 1024], F32))
        psB = st0.enter_context(nc.psum_tensor("psB", [128, 1024], F32))
        ps4 = st0.enter_context(nc.psum_tensor("ps4", [128, 512], F32))
        ps5 = st0.enter_context(nc.psum_tensor("ps5", [128, 512], F32))
        ps6 = st0.enter_context(nc.psum_tensor("ps6", [128, 512], F32))
        ps7 = st0.enter_context(nc.psum_tensor("ps7", [128, 1024], BF16))
        ACC = [Reg(psA[:, 0:512], True), Reg(psA[:, 512:1024], True), Reg(psB[:, 0:512], True), Reg(psB[:, 512:1024], True),
               Reg(ps4[:, :], True), Reg(ps5[:, :], True)]
        import os
        if os.environ.get("ACC45") == "1":
            ACC = [ACC[4], ACC[5]]
            acc_state_list = [0, 1]
        SC = [Reg(psA[:, 0:640], True), Reg(psB[:, 0:640], True)]
        MSr = Reg(ps6[:, :], True)
        MS = [MSr[:, i * 128:(i + 1) * 128] for i in range(4)]
        TBr = Reg(ps7[:, :], True)
        TB = [TBr[:, 0:512], TBr[:, 512:1024]]
        acc_state = {"list": list(range(len(ACC))), "i": 0, "m": 0, "t": 0}

        def acc():
            lst = acc_state["list"]
            r = ACC[lst[acc_state["i"] % len(lst)]]
            acc_state["i"] += 1
            return r

        def msm():
            r = MS[acc_state["m"] % 4]
            acc_state["m"] += 1
            return r

        def tbb():
            r = TB[acc_state["t"] % 2]
            acc_state["t"] += 1
            return r

        cst = gsb([128, CW], F32)
        par = gsb([128, PW], F32)
        ident32 = cst[:, K_ID:K_ID + 128]
        ut32 = cst[:, K_UT:K_UT + 128]
        negi4 = cst[:, K_NI4:K_NI4 + 512]
        negs4 = cst[:, K_NS4:K_NS4 + 512]
        rmask = cst[:, K_RM:K_RM + 512]
        ones32 = cst[:, K_ONE:K_ONE + 128]
        cbf = gsb([128, 256], BF16)
        ident_bf = cbf[:, 0:128]
        ones_bf = cbf[:, 128:256]
        der = gsb([128, 16], F32)
        wgu = gsb([16, depth * 512], BF16)
        xTt = st0.enter_context(nc.sbuf_tensor("xT", [128, KC, T], F32))
        xT = [Reg(xTt[:, k, :]) for k in range(KC)]
        hTt = st0.enter_context(nc.sbuf_tensor("hT", [128, KC, T], BF16))
        hT = [Reg(hTt[:, k, :]) for k in range(KC)]
        NWST = 5
        wst = [gsb([128, WST], BF16) for _ in range(NWST)]
        wst_i = [0]
        sqb = [gsb([128, T], BF16) for _ in range(2)]
        rs = gsb([128, T], F32)
        bmh = [gsb([128, 640], F32) for _ in range(2)]
        ogla = gsb([128, 4, T], BF16)
        oatt = gsb([128, 4, T], BF16)
        ogdn = gsb([128, 4, T], BF16)
        kTh = [gsb([128, 4, 2 * T], BF16) for _ in range(depth)]
        vh = [gsb([128, 8, 8 * 65], BF16) for _ in range(depth)]
        Sgla = [gsb([128, 512], F32) for _ in range(depth)]
        Sgla_b = [gsb([128, 512], BF16) for _ in range(depth)]
        Sgdn = [gsb([128, 512], F32) for _ in range(depth)]
        Sgdn_b = [gsb([128, 512], BF16) for _ in range(depth)]
        chist = [gsb([128, 12, 4], BF16) for _ in range(depth)]

        S.dma("sp", out=cst.v, in_=cst_d)
        S.dma("sp", out=par.v, in_=par_d)
        S.dma_once("pool", out=wgu.v, in_=wgu_d)
        S.I("dve", "tensor_copy", out=cbf[:, 0:128], in_=ident32)
        S.I("dve", "tensor_copy", out=cbf[:, 128:256], in_=ones32)
        S.I("dve", "tensor_scalar", out=der[:, 0:8], in0=par[:, P_GLAB:P_GLAB + 8], scalar1=-1.0, scalar2=None, op0=ALU.mult)
        S.I("act", "activation", out=der[:, 8:16], in_=par[:, P_ALOG:P_ALOG + 8], func=AF.Exp)
        S.I("dve", "tensor_scalar", out=der[:, 8:16], in0=der[:, 8:16], scalar1=-1.0, scalar2=None, op0=ALU.mult)
        for l in range(depth):
            S.I("pool", "memset", ap=vh[l].v, constant=1.0)
            S.I("pool", "memset", ap=Sgla[l].v, constant=0.0)
            S.I("pool", "memset", ap=Sgla_b[l].v, constant=0.0)
            S.I("pool", "memset", ap=Sgdn[l].v, constant=0.0)
            S.I("pool", "memset", ap=Sgdn_b[l].v, constant=0.0)
            S.I("pool", "memset", ap=chist[l].v, constant=0.0)
            S.I("pool", "memset", ap=kTh[l].v, constant=0.0)

        wb = {}

        def cast(name, src, dst, rows):
            for l in range(depth):
                b = Reg(None)
                S.dma_once("pool", out=dst[l], in_=src[l], xw=[b])
                wb[(name, l)] = [b]

        import os
        NOCAST = os.environ.get("NOCAST") == "1"
        if NOCAST:
            def cast(name, src, dst, rows):
                for l in range(depth):
                    wb[(name, l)] = []
        cast("in", w_in_d, w_in_b, D)
        cast("bg", w_bg_d, w_bg_b, 512)
        cast("ba", w_ba_d, w_ba_b, 512)
        cast("bd", w_bd_d, w_bd_b, 512)
        cast("out", w_out_d, w_out_b, D)
        cast("fi", w_fi_d, w_fi_b, D)
        cast("fo", w_fo_d, w_fo_b, FH)

        def load(name, l, view, a, n):
            r = wst[wst_i[0] % NWST]
            wst_i[0] += 1
            dst = r[:, 0:a * n].re("p (a n) -> p a n", a=a)
            S.dma("sp", out=dst, in_=view, xr=wb[(name, l)])
            return dst

        def load_in(l, c0, n):
            return load("in", l, w_in_b[l].rearrange("(k p) c -> p k c", p=128)[:, :, c0:c0 + n], KC, n)

        def proj_fm(a, wv, m=128):
            for kc in range(KC):
                S.I("pe", "matmul", out=a[0:m, :], lhsT=wv[:, kc, :], rhs=hT[kc].v, start=(kc == 0), stop=(kc == KC - 1))

        def proj_tm(a, tt, wv, n):
            for kc in range(KC):
                S.I("pe", "matmul", out=a[:, 0:n], lhsT=hT[kc][:, tt * 128:(tt + 1) * 128], rhs=wv[:, kc, :],
                    start=(kc == 0), stop=(kc == KC - 1))

        def rmsnorm(gcol0):
            a = acc()
            for kc in range(KC):
                q = sqb[kc % 2]
                S.I("act", "activation", out=q.v, in_=xT[kc].v, func=AF.Square)
                S.I("pe", "matmul", out=a.v, lhsT=ones_bf, rhs=q.v, start=(kc == 0), stop=(kc == KC - 1))
            S.I("act", "activation", out=rs.v, in_=a.v, func=AF.Ln, scale=1.0 / D, bias=EPS)
            S.I("act", "activation", out=rs.v, in_=rs.v, func=AF.Exp, scale=-0.5)

        def headnorm(raw, gate_v, gcol, out_v, accfn=None):
            q = sqb[0]
            a = (accfn or acc)()
            S.I("act", "activation", out=q.v, in_=raw.v, func=AF.Square)
            S.I("pe", "matmul", out=a.v, lhsT=ones_bf, rhs=q.v, start=True, stop=True)
            S.I("act", "activation", out=rs.v, in_=a.v, func=AF.Ln, scale=1.0 / 128, bias=EPS)
            S.I("act", "activation", out=rs.v, in_=rs.v, func=AF.Exp, scale=-0.5)
            S.I("dve", "scalar_tensor_tensor", out=rs.v, in0=raw.v, scalar=gcol, in1=rs.v, op0=ALU.mult, op1=ALU.mult)
            S.I("pool", "tensor_tensor", out=out_v, in0=rs.v.re("p (h t) -> p h t", h=4), in1=gate_v, op=ALU.mult)

        def gla_phase(l, st):
            with ExitStack() as ph:
                lrT = sb(ph, [16, T], BF16)
                gb = [[sb(ph, [128, T], F32) for _ in range(5)] for _ in range(2)]
                qg = sb(ph, [128, 4, T], BF16)
                kg = sb(ph, [128, 4, T], BF16)
                vtok = [sb(ph, [128, 512], BF16) for _ in range(4)]
                rsl = sb(ph, [128, 4, T], BF16)
                egl = sb(ph, [128, 16], F32)
                sTm = [sb(ph, [128, 4, 128], BF16) for _ in range(2)]
                kkT = [sb(ph, [128, 4, 128], BF16) for _ in range(2)]
                kkt = [sb(ph, [128, 4, 128], BF16) for _ in range(2)]

                wl = load_in(l, C_LR, 16)
                a = acc()
                proj_fm(a, wl, m=16)
                S.I("act", "activation", out=lrT.v, in_=a[0:16, :], func=AF.Copy)
                wq = load_in(l, C_GQ, 512)
                wk = load_in(l, C_GK, 512)
                wv = load_in(l, C_GV, 512)
                wr = load_in(l, C_GR, 512)
                gpool = [0, 1]

                def gate_gen(h):
                    while not gpool:
                        yield
                    sl = gpool.pop(0)
                    t1, t2, Gs, eg, eng = (gb[sl][k] for k in range(5))
                    a = acc()
                    S.I("pe", "matmul", out=a.v, lhsT=wgu[0:16, l * 512 + h * 128:l * 512 + (h + 1) * 128], rhs=lrT.v,
                        start=True, stop=True)
                    S.I("act", "activation", out=t1.v, in_=a.v, func=AF.Exp, scale=-1.0, bias=der[:, l * 4 + h:l * 4 + h + 1])
                    yield
                    S.I("act", "activation", out=t2.v, in_=t1.v, func=AF.Ln, bias=1.0)
                    yield
                    S.I("dve", "tensor_tensor_scan", out=Gs.v, data0=rmask, data1=t2.v, initial=0.0, op0=ALU.mult, op1=ALU.add)
                    yield
                    S.I("act", "activation", out=eg.v, in_=Gs.v, func=AF.Exp, scale=-1.0 / 16)
                    S.I("act", "activation", out=eng.v, in_=Gs.v, func=AF.Exp, scale=1.0 / 16)
                    S.I("act", "activation", out=egl[:, h * 4:(h + 1) * 4], in_=Gs.v.re("p (t c) -> p t c", c=128)[:, :, 127],
                        func=AF.Exp, scale=-1.0 / 16)
                    yield
                    a = acc()
                    proj_fm(a, wq[:, :, h * 128:(h + 1) * 128])
                    S.I("dve", "scalar_tensor_tensor", out=qg[:, h, :], in0=a.v, scalar=128 ** -0.5, in1=eg.v, op0=ALU.mult, op1=ALU.mult)
                    yield
                    a = acc()
                    proj_fm(a, wk[:, :, h * 128:(h + 1) * 128])
                    S.I("dve", "tensor_tensor", out=kg[:, h, :], in0=a.v, in1=eng.v, op=ALU.mult)
                    gpool.append(sl)

                def v_gen(tt):
                    a = acc()
                    proj_tm(a, tt, wv, 512)
                    S.I("act", "activation", out=vtok[tt].v, in_=a.v, func=AF.Copy)
                    yield

                def r_gen(h):
                    a = acc()
                    proj_fm(a, wr[:, :, h * 128:(h + 1) * 128])
                    S.I("act", "activation", out=rsl[:, h, :], in_=a.v, func=AF.Silu)
                    yield

                run_interleaved([v_gen(0), gate_gen(0), gate_gen(1), v_gen(1), r_gen(0), v_gen(2), gate_gen(2), gate_gen(3), r_gen(1),
                                 v_gen(3), r_gen(2), r_gen(3)], 3)
                utb = ut32.un(1).bc([128, 4, 128])
                eglv = egl.v.re("p (h t) -> p h t", t=4)
                s3 = Sgla[l].v.re("p (h d) -> p h d", h=4)

                def r4(a_):
                    return a_.v.re("p (h t) -> p h t", h=4)

                def pre(tt):
                    ts = slice(tt * 128, (tt + 1) * 128)
                    sc = acc()
                    for h in range(4):
                        S.I("pe", "matmul", out=sc[:, h * 128:(h + 1) * 128], lhsT=kg[:, h, ts], rhs=qg[:, h, ts], start=True, stop=True)
                    S.I("dve", "tensor_tensor", out=sTm[tt % 2].v, in0=r4(sc), in1=utb, op=ALU.mult)
                    S.I("pool", "tensor_tensor", out=kkT[tt % 2].v, in0=kg[:, :, ts], in1=eglv[:, :, tt].un(2).bc([128, 4, 128]), op=ALU.mult)
                    tb = tbb()
                    for h in range(4):
                        S.I("pe", "transpose", out=tb[:, h * 128:(h + 1) * 128], in_=kkT[tt % 2][:, h, :], identity=ident_bf)
                    S.I("act", "activation", out=kkt[tt % 2].v, in_=tb.v.re("p (h t) -> p h t", h=4), func=AF.Copy)

                def post(tt):
                    ts = slice(tt * 128, (tt + 1) * 128)
                    oa = acc()
                    for h in range(4):
                        hs = slice(h * 128, (h + 1) * 128)
                        S.I("pe", "matmul", out=oa[:, hs], lhsT=vtok[tt][:, hs], rhs=sTm[tt % 2][:, h, :], start=True, stop=False)
                        S.I("pe", "matmul", out=oa[:, hs], lhsT=Sgla_b[l][:, hs], rhs=qg[:, h, ts], start=False, stop=True)
                    pm = acc()
                    for h in range(4):
                        hs = slice(h * 128, (h + 1) * 128)
                        S.I("pe", "matmul", out=pm[:, hs], lhsT=kkt[tt % 2][:, h, :], rhs=vtok[tt][:, hs], start=True, stop=True)
                    S.I("dve", "tensor_tensor", out=s3, in0=s3, in1=eglv[:, :, tt].un(2).bc([128, 4, 128]), op=ALU.mult)
                    S.I("dve", "tensor_tensor", out=Sgla[l].v, in0=pm.v, in1=Sgla[l].v, op=ALU.add)
                    S.I("act", "activation", out=Sgla_b[l].v, in_=Sgla[l].v, func=AF.Copy)
                    headnorm(oa, rsl[:, :, ts], par[:, P_GLANG + l:P_GLANG + l + 1], ogla[:, :, ts])

                pre(0)
                for tt in range(4):
                    if tt + 1 < 4:
                        pre(tt + 1)
                    post(tt)
                if st == 0:
                    dump("ogla%d" % l, ogla.v, [128, 4, T])

        def att_phase(l, st):
            with ExitStack() as ph:
                qT = sb(ph, [128, 4, T], BF16)
                tmp = [sb(ph, [128, 640], F32) for _ in range(2)]
                pT = [sb(ph, [128, 640], BF16) for _ in range(2)]
                otok = sb(ph, [128, 4, 512], BF16)
                rden = sb(ph, [128, 4], F32)
                slot = st % 2
                wq = load_in(l, C_AQ, 512)
                for c in range(4):
                    a = acc()
                    proj_fm(a, wq[:, :, c * 128:(c + 1) * 128])
                    S.I("act", "activation", out=qT[:, c, :], in_=a.v, func=AF.Copy, scale=0.125)
                wk = load_in(l, C_AK, 512)
                for c in range(4):
                    a = acc()
                    proj_fm(a, wk[:, :, c * 128:(c + 1) * 128])
                    S.I("dve", "tensor_copy", out=kTh[l][:, c, slot * T:(slot + 1) * T], in_=a.v)
                wv = load_in(l, C_AV, 512)
                for tt in range(4):
                    m = st * 4 + tt
                    a = acc()
                    proj_tm(a, tt, wv, 512)
                    S.I("act", "activation", out=vh[l][:, m % 8, :].re("p (h e) -> p h e", e=65)[:, :, 0:64],
                        in_=a.v.re("p (h e) -> p h e", e=64), func=AF.Copy)
                S.fresh(SC)
                acc_state["list"] = [4, 5]
                pvs = {}

                def qk_exp(h, tt):
                    c, pb = h // 2, (h % 2) * 64
                    bmr = bmh[h % 2]
                    if tt == 0:
                        S.dma("sp", out=bmr.v, in_=bm_d[l][:, h * 640:(h + 1) * 640])
                    m = st * 4 + tt
                    kt0 = max(0, 4 - m)
                    ts = slice(tt * 128, (tt + 1) * 128)
                    sc = SC[tt % 2]
                    for kt in range(kt0, 5):
                        mk = m - 4 + kt
                        off = ((mk // 4) % 2) * T + (mk % 4) * 128
                        S.I("pe", "matmul", out=sc[:, kt * 128:(kt + 1) * 128], lhsT=kTh[l][pb:pb + 64, c, off:off + 128],
                            rhs=qT[pb:pb + 64, c, ts], start=True, stop=True)
                    cs = slice(kt0 * 128, 640)
                    tm, pt = tmp[tt % 2], pT[tt % 2]
                    S.I("dve", "tensor_tensor", out=tm[:, cs], in0=sc[:, cs], in1=bmr[:, cs], op=ALU.add)
                    S.I("act", "activation", out=pt[:, cs], in_=tm[:, cs], func=AF.Exp)

                def pv_part(h, tt):
                    if tt == 0:
                        pvs[h] = acc()
                    pv = pvs[h]
                    m = st * 4 + tt
                    kt0 = max(0, 4 - m)
                    pt = pT[tt % 2]
                    for kt in range(kt0, 5):
                        mk = m - 4 + kt
                        S.I("pe", "matmul", out=pv[:, tt * 65:(tt + 1) * 65], lhsT=pt[:, kt * 128:(kt + 1) * 128],
                            rhs=vh[l][:, mk % 8, h * 65:(h + 1) * 65], start=(kt == kt0), stop=(kt == 4))
                    if tt == 3:
                        pv3 = pv[:, 0:260].re("p (t e) -> p t e", e=65)
                        S.I("dve", "reciprocal", out=rden.v, in_=pv3[:, :, 64])
                        S.I("dve", "tensor_tensor", out=otok[:, :, h * 64:(h + 1) * 64], in0=pv3[:, :, 0:64],
                            in1=rden.v.un(2).bc([128, 4, 64]), op=ALU.mult)

                its = [(h, tt) for h in range(8) for tt in range(4)]
                qk_exp(*its[0])
                for i in range(len(its)):
                    if i + 1 < len(its):
                        qk_exp(*its[i + 1])
                    pv_part(*its[i])
                for tt in range(4):
                    ts = slice(tt * 128, (tt + 1) * 128)
                    tb = tbb()
                    for c in range(4):
                        S.I("pe", "transpose", out=tb[:, c * 128:(c + 1) * 128], in_=otok[:, tt, c * 128:(c + 1) * 128], identity=ident_bf)
                    S.I("act", "activation", out=oatt[:, :, ts], in_=tb.v.re("p (c t) -> p c t", c=4), func=AF.Copy)
                acc_state["list"] = list(range(6))
                S.fresh(ACC[0:4])
                if st == 0:
                    dump("oatt%d" % l, oatt.v, [128, 4, T])

        def gdn_phase(l, st):
            with ExitStack() as ph:
                qn = [sb(ph, [128, T], BF16) for _ in range(4)]
                kn = [sb(ph, [128, T], BF16) for _ in range(4)]
                vT = [sb(ph, [128, T], BF16) for _ in range(4)]
                zs = sb(ph, [128, 4, T], BF16)
                cv = ExitStack()
                NSL = 3
                cin = [sb(cv, [128, T + 4], BF16) for _ in range(NSL)]
                cacc = [sb(cv, [128, T], F32) for _ in range(NSL)]
                csq = [sb(cv, [128, T], BF16) for _ in range(NSL)]
                crs = [sb(cv, [128, T], F32) for _ in range(NSL)]
                cdg = [sb(cv, [128, 4, 128], BF16) for _ in range(NSL)]
                cw = par[:, P_CONV + l * 48:P_CONV + (l + 1) * 48]
                wts = {}

                def get_w(kind):
                    if kind not in wts:
                        wts[kind] = load_in(l, {"q": C_DQ, "k": C_DK, "v": C_DV, "z": C_DZ}[kind], 512)
                    return wts[kind]

                def conv_gen(ci, slot):
                    kind = "qkv"[ci // 4]
                    h = ci % 4
                    wt = get_w(kind)
                    a = acc()
                    proj_fm(a, wt[:, :, h * 128:(h + 1) * 128])
                    x_ = cin[slot]
                    S.I("act", "activation", out=x_[:, 4:T + 4], in_=a.v, func=AF.Copy)
                    S.I("pool", "tensor_copy", out=x_[:, 0:4], in_=chist[l][:, ci, :])
                    dg = cdg[slot]
                    S.I("pool", "tensor_tensor", out=dg.v, in0=ident_bf.un(1).bc([128, 4, 128]),
                        in1=cw[:, ci * 4:ci * 4 + 4].un(2).bc([128, 4, 128]), op=ALU.mult)
                    yield
                    o = cacc[slot]
                    a = acc()
                    for i in range(4):
                        S.I("pe", "matmul", out=a.v, lhsT=dg[:, i, :], rhs=x_[:, 1 + i:T + 1 + i], start=(i == 0), stop=(i == 3))
                    S.I("pool", "tensor_copy", out=chist[l][:, ci, :], in_=x_[:, T:T + 4])
                    if kind == "v":
                        S.I("act", "activation", out=vT[h].v, in_=a.v, func=AF.Silu)
                        return
                    S.I("act", "activation", out=o.v, in_=a.v, func=AF.Silu)
                    yield
                    q_, r_ = csq[slot], crs[slot]
                    a = acc()
                    S.I("act", "activation", out=q_.v, in_=o.v, func=AF.Square)
                    S.I("pe", "matmul", out=a.v, lhsT=ones_bf, rhs=q_.v, start=True, stop=True)
                    S.I("act", "activation", out=r_.v, in_=a.v, func=AF.Ln, scale=1.0, bias=EPS)
                    S.I("act", "activation", out=r_.v, in_=r_.v, func=AF.Exp, scale=-0.5)
                    yield
                    dst = qn[h].v if kind == "q" else kn[h].v
                    S.I("dve", "scalar_tensor_tensor", out=dst, in0=o.v, scalar=(128 ** -0.5 if kind == "q" else 1.0), in1=r_.v,
                        op0=ALU.mult, op1=ALU.add if False else ALU.mult)

                def z_gen(h):
                    wz = get_w("z")
                    a = acc()
                    proj_fm(a, wz[:, :, h * 128:(h + 1) * 128])
                    S.I("act", "activation", out=zs[:, h, :], in_=a.v, func=AF.Silu)
                    yield

                slots = list(range(NSL))

                def slotted(ci):
                    sl = slots.pop(0)
                    yield from conv_gen(ci, sl)
                    slots.append(sl)

                run_interleaved([slotted(ci) for ci in range(12)] + [z_gen(h) for h in range(4)], NSL)
                wab = load_in(l, C_AB, 8)
                cv.close()

                def mkbufs():
                    Bf = {}
                    Bf["sc_"] = sb(ph, [128, 64], F32)
                    for nm in ("M1", "M1b"):
                        Bf[nm] = sb(ph, [128, 4, 128], F32)
                    for nm in ("ET", "ETb", "Bm", "Am", "Y"):
                        Bf[nm] = sb(ph, [128, 4, 128], NEU_DT)
                    for nm in ("Yb", "attnT", "kbg", "kkt", "vbt", "qgT", "nwT", "vnew"):
                        Bf[nm] = sb(ph, [128, 4, 128], BF16)
                    return Bf

                bufsets = [mkbufs(), mkbufs()]
                idb = ident32.un(1).bc([128, 4, 128])
                utb = ut32.un(1).bc([128, 4, 128])
                turn = [0]

                def b4(v):
                    return v.un(2).bc([128, 4, 128])

                def r4(a):
                    return a.v.re("p (h t) -> p h t", h=4)

                def gdn_tile(tt, Bf):
                    own = [0, 1, 2] if tt % 2 == 0 else [3, 4, 5]
                    ctr = [0]

                    def A():
                        r = ACC[own[ctr[0] % 3]]
                        ctr[0] += 1
                        return r

                    sc_, M1, M1b, ET, ETb, Bm, Am, Y = (Bf[k] for k in ("sc_", "M1", "M1b", "ET", "ETb", "Bm", "Am", "Y"))
                    Yb, attnT, kbg, kkt, vbt, qgT, nwT, vnew = (Bf[k] for k in ("Yb", "attnT", "kbg", "kkt", "vbt", "qgT", "nwT", "vnew"))
                    EGR, P2, Q2 = M1b, ET, ETb
                    ts = slice(tt * 128, (tt + 1) * 128)
                    a = msm()
                    proj_tm(a, tt, wab, 8)
                    la, lb, gc, gl = sc_[:, 0:4], sc_[:, 4:8], sc_[:, 8:12], sc_[:, 12:16]
                    S.I("dve", "tensor_tensor", out=sc_[:, 16:20], in0=a[:, 0:4], in1=par[:, P_DTB + l * 4:P_DTB + l * 4 + 4], op=ALU.add)
                    S.I("act", "activation", out=sc_[:, 20:24], in_=a[:, 4:8], func=AF.Exp, scale=-1.0)
                    yield
                    S.I("act", "activation", out=sc_[:, 16:20], in_=sc_[:, 16:20], func=AF.Exp)
                    S.I("act", "activation", out=sc_[:, 16:20], in_=sc_[:, 16:20], func=AF.Ln, bias=1.0)
                    S.I("act", "activation", out=lb, in_=sc_[:, 20:24], func=AF.Ln, bias=1.0)
                    yield
                    S.I("dve", "tensor_tensor", out=la, in0=sc_[:, 16:20], in1=der[:, 8 + l * 4:12 + l * 4], op=ALU.mult)
                    S.I("dve", "tensor_scalar", out=sc_[:, 36:40], in0=lb, scalar1=-1.0, scalar2=None, op0=ALU.mult)
                    nlb = sc_[:, 36:40]
                    S.I("dve", "tensor_tensor", out=M1.v, in0=utb, in1=b4(la), op=ALU.mult)
                    S.I("pool", "tensor_tensor", out=M1b.v, in0=idb, in1=b4(nlb), op=ALU.mult)
                    S.I("pool", "tensor_tensor", out=M1b.v, in0=M1b.v, in1=M1.v, op=ALU.add)
                    yield
                    g2 = msm()
                    S.I("pe", "matmul", out=g2[:, 0:4], lhsT=ut32, rhs=la, start=True, stop=True)
                    S.I("pe", "matmul", out=g2[:, 4:8], lhsT=ones32, rhs=la, start=True, stop=True)
                    S.I("dve", "tensor_copy", out=sc_[:, 8:16], in_=g2[:, 0:8])
                    m1f = M1.v.re("p h t -> p (h t)")
                    m1bf = M1b.v.re("p h t -> p (h t)")
                    d1, d2, gr = A(), A(), A()
                    S.I("pe", "matmul", out=d1.v, lhsT=ones32, rhs=m1f, start=True, stop=False)
                    S.I("pe", "matmul", out=d1.v, lhsT=ident32, rhs=negi4, start=False, stop=True)
                    S.I("pe", "matmul", out=d2.v, lhsT=ones32, rhs=m1bf, start=True, stop=False)
                    S.I("pe", "matmul", out=d2.v, lhsT=ident32, rhs=negs4, start=False, stop=True)
                    S.I("pe", "matmul", out=gr.v, lhsT=ones32, rhs=m1f, start=True, stop=True)
                    yield
                    S.I("dve", "tensor_tensor", out=sc_[:, 24:28], in0=gc, in1=lb, op=ALU.subtract)
                    S.I("dve", "tensor_tensor", out=sc_[:, 28:32], in0=gl, in1=gc, op=ALU.subtract)
                    S.I("dve", "tensor_copy", out=sc_[:, 32:36], in_=gl)
                    S.I("act", "activation", out=sc_[:, 40:56], in_=sc_[:, 24:40], func=AF.Exp)
                    ckbg, ckk, egl, beta = sc_[:, 40:44], sc_[:, 44:48], sc_[:, 48:52], sc_[:, 52:56]
                    S.I("dve", "tensor_tensor", out=ET.v, in0=r4(d1), in1=b4(gc), op=ALU.subtract)
                    S.I("act", "activation", out=ET.v, in_=ET.v, func=AF.Exp)
                    yield
                    S.I("dve", "tensor_tensor", out=ETb.v, in0=r4(d2), in1=b4(gc), op=ALU.subtract)
                    S.I("act", "activation", out=ETb.v, in_=ETb.v, func=AF.Exp)
                    S.I("act", "activation", out=EGR.v, in_=r4(gr), func=AF.Exp)
                    yield
                    kk_, qk_ = A(), A()
                    for h in range(4):
                        hs = slice(h * 128, (h + 1) * 128)
                        S.I("pe", "matmul", out=kk_[:, hs], lhsT=kn[h][:, ts], rhs=kn[h][:, ts], start=True, stop=True)
                        S.I("pe", "matmul", out=qk_[:, hs], lhsT=kn[h][:, ts], rhs=qn[h][:, ts], start=True, stop=True)
                    yield
                    S.I("dve", "tensor_tensor", out=Bm.v, in0=r4(kk_), in1=ETb.v, op=ALU.mult)
                    S.I("dve", "tensor_tensor", out=attnT.v, in0=r4(qk_), in1=ET.v, op=ALU.mult)
                    at = A()
                    for h in range(4):
                        S.I("pe", "transpose", out=at[:, h * 128:(h + 1) * 128], in_=V(Bm, Bm.ap[:, h, :].bitcast(F32)), identity=ident32)
                    yield
                    S.I("act", "activation", out=Am.v, in_=r4(at), func=AF.Copy)
                    S.I("pool", "tensor_tensor", out=Y.v, in0=idb, in1=Bm.v, op=ALU.subtract)
                    for h in range(4):
                        S.I("pool", "tensor_tensor", out=qgT[:, h, :], in0=qn[h][:, ts], in1=EGR[:, h, :], op=ALU.mult)
                    yield
                    Pc, Qc, Pn, Qn = Am, Bm, P2, Q2
                    for k in range(1, 7):
                        pp = A()
                        for h in range(4):
                            S.I("pe", "matmul", out=pp[:, h * 128:(h + 1) * 128], lhsT=Qc[:, h, :], rhs=Pc[:, h, :], start=True, stop=True)
                        qq = A()
                        if k < 6:
                            for h in range(4):
                                S.I("pe", "matmul", out=qq[:, h * 128:(h + 1) * 128], lhsT=Pc[:, h, :], rhs=Qc[:, h, :], start=True, stop=True)
                        yield
                        S.I("act", "activation", out=Pn.v, in_=r4(pp), func=AF.Copy)
                        if k < 6:
                            S.I("dve", "tensor_copy", out=Qn.v, in_=r4(qq))
                        yy = A()
                        for h in range(4):
                            S.I("pe", "matmul", out=yy[:, h * 128:(h + 1) * 128], lhsT=Pn[:, h, :], rhs=Y[:, h, :], start=True, stop=True)
                        yield
                        S.I("dve", "tensor_tensor", out=Y.v, in0=r4(yy), in1=Y.v, op=ALU.add)
                        Pc, Pn = Pn, Pc
                        Qc, Qn = Qn, Qc
                    S.I("act", "activation", out=Yb.v, in_=Y.v, func=AF.Copy)
                    yield
                    tk, tv = tbb(), tbb()
                    for h in range(4):
                        S.I("pe", "transpose", out=tk[:, h * 128:(h + 1) * 128], in_=kn[h][:, ts], identity=ident_bf)
                        S.I("pe", "transpose", out=tv[:, h * 128:(h + 1) * 128], in_=vT[h][:, ts], identity=ident_bf)
                    S.I("dve", "tensor_tensor", out=kbg.v, in0=r4(tk), in1=b4(ckbg), op=ALU.mult)
                    S.I("dve", "tensor_tensor", out=kkt.v, in0=r4(tk), in1=b4(ckk), op=ALU.mult)
                    S.I("dve", "tensor_tensor", out=vbt.v, in0=r4(tv), in1=b4(beta), op=ALU.mult)
                    yield
                    ww = A()
                    for h in range(4):
                        S.I("pe", "matmul", out=ww[:, h * 128:(h + 1) * 128], lhsT=kbg[:, h, :], rhs=Yb[:, h, :], start=True, stop=True)
                    yield
                    S.I("act", "activation", out=nwT.v, in_=r4(ww), func=AF.Copy, scale=-1.0)
                    while turn[0] != tt:
                        yield
                    vn = A()
                    for h in range(4):
                        hs = slice(h * 128, (h + 1) * 128)
                        S.I("pe", "matmul", out=vn[:, hs], lhsT=Yb[:, h, :], rhs=vbt[:, h, :], start=True, stop=False)
                        S.I("pe", "matmul", out=vn[:, hs], lhsT=nwT[:, h, :], rhs=Sgdn_b[l][:, hs], start=False, stop=True)
                    S.I("act", "activation", out=vnew.v, in_=r4(vn), func=AF.Copy)
                    oo = A()
                    for h in range(4):
                        hs = slice(h * 128, (h + 1) * 128)
                        S.I("pe", "matmul", out=oo[:, hs], lhsT=Sgdn_b[l][:, hs], rhs=qgT[:, h, :], start=True, stop=False)
                        S.I("pe", "matmul", out=oo[:, hs], lhsT=vnew[:, h, :], rhs=attnT[:, h, :], start=False, stop=True)
                    pst = A()
                    for h in range(4):
                        hs = slice(h * 128, (h + 1) * 128)
                        S.I("pe", "matmul", out=pst[:, hs], lhsT=kkt[:, h, :], rhs=vnew[:, h, :], start=True, stop=True)
                    s3 = Sgdn[l].v.re("p (h t) -> p h t", h=4)
                    S.I("dve", "tensor_tensor", out=s3, in0=s3, in1=b4(egl), op=ALU.mult)
                    S.I("dve", "tensor_tensor", out=Sgdn[l].v, in0=pst.v, in1=Sgdn[l].v, op=ALU.add)
                    S.I("act", "activation", out=Sgdn_b[l].v, in_=Sgdn[l].v, func=AF.Copy)
                    turn[0] += 1
                    yield
                    headnorm(oo, zs[:, :, ts], par[:, P_GDNNG + l:P_GDNNG + l + 1], ogdn[:, :, ts], accfn=A)

                run_interleaved([gdn_tile(tt, bufsets[tt % 2]) for tt in range(4)], 2, lead=GDN_LEAD)
                if st == 0:
                    dump("ogdn%d" % l, ogdn.v, [128, 4, T])

        def merge_phase(l, st):
            with ExitStack() as ph:
                gates = sb(ph, [128, 24, T], BF16)
                y = [sb(ph, [128, T], BF16) for _ in range(KC)]
                tg = [sb(ph, [128, T], F32) for _ in range(3)]
                for j in range(6):
                    wg = load_in(l, C_MG + j * 512, 512)
                    for cc in range(4):
                        a = acc()
                        proj_fm(a, wg[:, :, cc * 128:(cc + 1) * 128])
                        S.I("act", "activation", out=gates[:, j * 4 + cc, :], in_=a.v, func=AF.Sigmoid)
                wbr = []
                for name, wd in (("bg", w_bg_b), ("ba", w_ba_b), ("bd", w_bd_b)):
                    wbr.append(load(name, l, wd[l].rearrange("(c p) d -> p c d", p=128), 4, D))
                srcs = [[ogla[:, h, :] for h in range(4)], [oatt[:, c, :] for c in range(4)], [ogdn[:, h, :] for h in range(4)]]
                for dc in range(KC):
                    ds = slice(dc * 128, (dc + 1) * 128)
                    for b in range(3):
                        a = acc()
                        for c in range(4):
                            S.I("pe", "matmul", out=a.v, lhsT=wbr[b][:, c, ds], rhs=srcs[b][c], start=(c == 0), stop=(c == 3))
                        S.I("dve", "tensor_tensor", out=tg[b].v, in0=a.v, in1=gates[:, b * 8 + dc, :], op=ALU.mult)
                    S.I("pool", "tensor_tensor", out=tg[0].v, in0=tg[0].v, in1=tg[1].v, op=ALU.add)
                    S.I("pool", "tensor_tensor", out=y[dc].v, in0=tg[0].v, in1=tg[2].v, op=ALU.add)
                if st == 0:
                    for dc in range(KC):
                        dump("y%d" % l, y[dc].v, [128, T])
                for half in range(2):
                    wo = load("out", l, w_out_b[l].rearrange("(c p) d -> p c d", p=128)[:, :, half * 512:(half + 1) * 512], KC, 512)
                    for dd in range(4):
                        dc2 = half * 4 + dd
                        a = acc()
                        for dc in range(KC):
                            S.I("pe", "matmul", out=a.v, lhsT=wo[:, dc, dd * 128:(dd + 1) * 128], rhs=y[dc].v, start=(dc == 0), stop=(dc == KC - 1))
                        S.I("dve", "tensor_tensor", out=xT[dc2].v, in0=a.v, in1=xT[dc2].v, op=ALU.add)

        def ffn_phase(l, st):
            with ExitStack() as ph:
                actb = [sb(ph, [128, T], BF16) for _ in range(22)]
                sg = [sb(ph, [128, T], F32) for _ in range(2)]
                rmsnorm_apply(P_FFNG + l * 8)
                for j in range(6):
                    n = 512 if j < 5 else 256
                    wg = load("fi", l, w_fi_b[l].rearrange("(k p) c -> p k c", p=128)[:, :, j * 512:j * 512 + n], KC, n)
                    wu = load("fi", l, w_fi_b[l].rearrange("(k p) c -> p k c", p=128)[:, :, FH + j * 512:FH + j * 512 + n], KC, n)
                    for cc in range(n // 128):
                        c = j * 4 + cc
                        ag = acc()
                        proj_fm(ag, wg[:, :, cc * 128:(cc + 1) * 128])
                        au = acc()
                        proj_fm(au, wu[:, :, cc * 128:(cc + 1) * 128])
                        s_ = sg[c % 2]
                        S.I("act", "activation", out=s_.v, in_=ag.v, func=AF.Silu)
                        S.I("dve", "tensor_tensor", out=actb[c].v, in0=au.v, in1=s_.v, op=ALU.mult)
                fov = w_fo_b[l].rearrange("(c p) d -> p c d", p=128)
                cgrp = [(0, 8), (8, 8), (16, 6)]
                for half in range(2):
                    wo = [load("fo", l, fov[:, c0:c0 + n, half * 512:(half + 1) * 512], n, 512) for c0, n in cgrp]
                    for dd in range(4):
                        dc2 = half * 4 + dd
                        a = acc()
                        for c in range(22):
                            S.I("pe", "matmul", out=a.v, lhsT=wo[c // 8][:, c % 8, dd * 128:(dd + 1) * 128], rhs=actb[c].v, start=(c == 0), stop=(c == 21))
                        S.I("dve", "tensor_tensor", out=xT[dc2].v, in0=a.v, in1=xT[dc2].v, op=ALU.add)

        def run_interleaved(gens, width, lead=0):
            it = iter(gens)
            active = []
            if lead:
                g0 = next(it)
                active.append(g0)
                for _ in range(lead):
                    next(g0)
            while True:
                while len(active) < width:
                    g = next(it, None)
                    if g is None:
                        break
                    active.append(g)
                if not active:
                    break
                for g in list(active):
                    try:
                        next(g)
                    except StopIteration:
                        active.remove(g)

        def rmsnorm_apply(gc0):
            rmsnorm(gc0)
            for kc in range(KC):
                S.I("dve", "scalar_tensor_tensor", out=hT[kc].v, in0=xT[kc].v,
                    scalar=par[:, gc0 + kc:gc0 + kc + 1], in1=rs.v, op0=ALU.mult, op1=ALU.mult)

        from contextlib import contextmanager

        @contextmanager
        def phase_mark(name):
            marks.append((name, "b", dict(S.seq)))
            yield
            marks.append((name, "e", dict(S.seq)))

        for st in range(NST if upto > 0 else 0):
            io_ph = ExitStack()
            xin = [sb(io_ph, [128, D], F32) for _ in range(2)]
            for tt in range(4):
                xi = xin[tt % 2]
                r0 = (st * 4 + tt) * 128
                S.dma("sp", out=xi.v, in_=x_d[r0:r0 + 128, :])
                for g in range(2):
                    a = acc()
                    for j in range(4):
                        kc = g * 4 + j
                        S.I("pe", "transpose", out=a[:, j * 128:(j + 1) * 128], in_=xi[:, kc * 128:(kc + 1) * 128], identity=ident32)
                    for j in range(4):
                        kc = g * 4 + j
                        S.I("act" if j % 2 == 0 else "dve", "activation" if j % 2 == 0 else "tensor_copy",
                            out=xT[kc][:, tt * 128:(tt + 1) * 128], in_=a[:, j * 128:(j + 1) * 128],
                            **({"func": AF.Copy} if j % 2 == 0 else {}))
            io_ph.close()
            for l in range(depth):
                if upto < 0.6:
                    break
                if upto < 0.8:
                    rmsnorm(P_MIXG + l * 8)
                    break
                rmsnorm_apply(P_MIXG + l * 8)
                if st == 0:
                    for kc in range(KC):
                        dump("h%d" % l, hT[kc].v, [128, T])
                if upto < 2:
                    break
                with phase_mark("gla"):
                    gla_phase(l, st)
                if upto < 3:
                    break
                with phase_mark("att"):
                    att_phase(l, st)
                if upto < 4:
                    break
                with phase_mark("gdn"):
                    gdn_phase(l, st)
                if upto < 5:
                    break
                with phase_mark("merge"):
                    merge_phase(l, st)
                if upto < 6:
                    break
                if st == 0:
                    for kc in range(KC):
                        dump("xmid%d" % l, xT[kc].v, [128, T])
                with phase_mark("ffn"):
                    ffn_phase(l, st)
                if st == 0:
                    for kc in range(KC):
                        dump("x%d" % l, xT[kc].v, [128, T])
            if upto < 7:
                continue
            io_ph = ExitStack()
            xin = [sb(io_ph, [128, D], F32) for _ in range(2)]
            rmsnorm(P_FING)
            for kc in range(KC):
                S.I("dve", "scalar_tensor_tensor", out=xT[kc].v, in0=xT[kc].v,
                    scalar=par[:, P_FING + kc:P_FING + kc + 1], in1=rs.v, op0=ALU.mult, op1=ALU.mult)
            for tt in range(4):
                xo = xin[tt % 2]
                r0 = (st * 4 + tt) * 128
                for g in range(2):
                    a = acc()
                    for j in range(4):
                        kc = g * 4 + j
                        S.I("pe", "transpose", out=a[:, j * 128:(j + 1) * 128], in_=xT[kc][:, tt * 128:(tt + 1) * 128], identity=ident32)
                    S.I("act" if g == 0 else "dve", "activation" if g == 0 else "tensor_copy",
                        out=xo[:, g * 512:(g + 1) * 512], in_=a.v, **({"func": AF.Copy} if g == 0 else {}))
                S.dma("sp", out=out_d[r0:r0 + 128, :], in_=xo.v)
            io_ph.close()
        S.finish()
        print("instructions:", S.ninstr, "waits:", S.nwait, "incs:", S.ninc, flush=True)
    return nc, dbg_out


def _host_consts():
    c = np.zeros((128, CW), np.float32)
    j = np.arange(128)[:, None]
    i = np.arange(128)[None, :]
    c[:, K_ID:K_ID + 128] = (j == i)
    ut = (j <= i).astype(np.float32)
    c[:, K_UT:K_UT + 128] = ut
    ni = np.where(j <= i, 0.0, NEG).astype(np.float32)
    ns = np.where(j < i, 0.0, NEG).astype(np.float32)
    c[:, K_NI4:K_NI4 + 512] = np.tile(ni, (1, 4))
    c[:, K_NS4:K_NS4 + 512] = np.tile(ns, (1, 4))
    rm = np.ones((128, 512), np.float32)
    rm[:, 0::128] = 0.0
    c[:, K_RM:K_RM + 512] = rm
    c[:, K_ONE:K_ONE + 128] = 1.0
    return c


def _host_params(inp, depth):
    p = np.zeros((128, PW), np.float32)

    def cols(v):
        return np.ascontiguousarray(v.reshape(-1, 128).T)

    for l in range(depth):
        p[:, P_MIXG + l * 8:P_MIXG + (l + 1) * 8] = cols(inp["mix_norm_g"][l])
        p[:, P_FFNG + l * 8:P_FFNG + (l + 1) * 8] = cols(inp["ffn_norm_g"][l])
        p[:, P_GLAB + l * 4:P_GLAB + (l + 1) * 4] = cols(inp["gla_b_gate"][l])
        p[:, P_GLANG + l] = inp["gla_norm_g"][l]
        p[:, P_GDNNG + l] = inp["gdn_norm_g"][l]
        cw = inp["gdn_conv_w"][l]
        p[:, P_CONV + l * 48:P_CONV + (l + 1) * 48] = cw.reshape(4, 12, 128).transpose(2, 1, 0).reshape(128, 48)
        p[:, P_ALOG + l * 4:P_ALOG + (l + 1) * 4] = np.broadcast_to(inp["gdn_a_log"][l][None, :], (128, 4))
        p[:, P_DTB + l * 4:P_DTB + (l + 1) * 4] = np.broadcast_to(inp["gdn_dt_bias"][l][None, :], (128, 4))
    p[:, P_FING:P_FING + 8] = cols(inp["final_norm_g"])
    return p


def _host_bias(rel_bias, depth):
    j = np.arange(128)[:, None, None]
    kt = np.arange(5)[None, :, None]
    i = np.arange(128)[None, None, :]
    rel = (4 - kt) * 128 + i - j
    idx = np.clip(rel, -63, 256) + 63
    dchunk = 2 * (kt - 4) + (j >= 64).astype(np.int64) - (i >= 64).astype(np.int64)
    valid = (dchunk >= -8) & (dchunk <= 0)
    idx = np.where(valid, idx, 320)
    out = np.zeros((depth, 128, 8, 640), np.float32)
    for l in range(depth):
        ext = np.concatenate([rel_bias[l], np.full((8, 1), NEG, np.float32)], axis=1)
        g = ext[:, idx]
        out[l] = g.transpose(1, 0, 2, 3).reshape(128, 8, 640)
    return out.reshape(depth, 128, 8 * 640)


def prepare_common(inp, depth=DEPTH):
    f = lambda a: np.ascontiguousarray(np.asarray(a, dtype=np.float32))
    com = {
        "w_in": f(inp["w_in"])[:depth], "w_branch_gla": f(inp["w_branch_gla"])[:depth],
        "w_branch_att": f(inp["w_branch_att"])[:depth], "w_branch_gdn": f(inp["w_branch_gdn"])[:depth],
        "w_out": f(inp["w_out"])[:depth], "w_ffn_in": f(inp["w_ffn_in"])[:depth], "w_ffn_out": f(inp["w_ffn_out"])[:depth],
        "wgu": np.ascontiguousarray(f(inp["gla_w_gate_up"])[:depth].transpose(1, 0, 2).reshape(16, depth * 512)),
        "cst": _host_consts(),
        "par": _host_params({k: f(v) for k, v in inp.items()}, depth),
        "bm": _host_bias(f(inp["att_rel_bias"]), depth),
    }
    return com


_CACHE = {}


def kernel(**inputs):
    x = np.asarray(inputs["x"], dtype=np.float32)
    B, SEQ, _ = x.shape
    NST = SEQ // T
    if NST not in _CACHE:
        _CACHE[NST] = build(NST)[0]
    nc = _CACHE[NST]
    com = prepare_common(inputs)
    in_maps = []
    for b in range(B):
        m = dict(com)
        m["x"] = np.ascontiguousarray(x[b])
        in_maps.append(m)
    res = run_bass_kernel_spmd(nc, in_maps, core_ids=list(range(B)))
    return np.stack([np.asarray(r["out"]) for r in res.results], axis=0).astype(np.float32)
```
